# Optimizing a Trainium2 kernel written in Bass

```python
import math
import jax, jax.numpy as jnp
from jax import lax
import numpy as np

D_MODEL = 1024
BATCH = 2
SEQ = 8192
DEPTH = 2

CHUNK = 64
N_MIXERS = 2
N_LAYERS_A = (DEPTH + N_MIXERS - 1) // N_MIXERS
N_LAYERS_B = DEPTH // N_MIXERS

A_HEADS = 8
A_HEAD_DIM = 128
A_WIDTH = A_HEADS * A_HEAD_DIM
CONV_WIDTH = 4
A_IN_COLS = 4 * A_WIDTH + 2 * A_HEADS

B_HEADS = 8
B_HEAD_DIM = 128
B_WIDTH = B_HEADS * B_HEAD_DIM
B_IN_COLS = 4 * B_WIDTH + B_HEADS
Q_BLOCK = 128

EPS = 1e-6

kernel_name = "hybrid_gdn_fox_interleaved"


def rms_norm(x, w):
    xf = x.astype(jnp.float32)
    y = xf * lax.rsqrt(jnp.mean(xf * xf, axis=-1, keepdims=True) + EPS)
    return (y * w.astype(jnp.float32)).astype(x.dtype)


def l2_normalize(x):
    xf = x.astype(jnp.float32)
    return xf * lax.rsqrt(jnp.sum(xf * xf, axis=-1, keepdims=True) + EPS)


def causal_depthwise_conv(x, w):
    c = x.shape[-1]
    return lax.conv_general_dilated(
        x, w[:, None, :].astype(x.dtype), window_strides=(1,),
        padding=[(CONV_WIDTH - 1, 0)], dimension_numbers=("NWC", "WIO", "NWC"),
        feature_group_count=c)


def chunk_gated_delta_rule(q, k, v, beta, g_log):
    b_, t_, h_, dk = q.shape
    dv = v.shape[-1]
    n = t_ // CHUNK

    def to_chunks(t):
        t = t.reshape((b_, n, CHUNK, h_) + t.shape[3:])
        return jnp.moveaxis(t, (1, 3), (0, 2))

    q = to_chunks(q) * (dk ** -0.5)
    k = to_chunks(k)
    v = to_chunks(v)
    beta = to_chunks(beta)
    g = jnp.cumsum(to_chunks(g_log), axis=-1)

    idx = jnp.arange(CHUNK)
    incl = idx[:, None] >= idx[None, :]
    strict = idx[:, None] > idx[None, :]
    gdiff = g[..., :, None] - g[..., None, :]
    decay_incl = jnp.exp(jnp.where(incl, gdiff, -jnp.inf))
    decay_strict = jnp.where(strict, decay_incl, 0.0)

    kb = k * beta[..., None]
    a_mat = jnp.einsum("nbhid,nbhjd->nbhij", kb, k) * decay_strict
    eye = jnp.eye(CHUNK, dtype=jnp.float32)
    t_mat = lax.linalg.triangular_solve(
        eye + a_mat, jnp.broadcast_to(eye, a_mat.shape),
        left_side=True, lower=True, unit_diagonal=True)
    u = t_mat @ (v * beta[..., None])
    w = t_mat @ (kb * jnp.exp(g)[..., None])
    attn_intra = jnp.einsum("nbhid,nbhjd->nbhij", q, k) * decay_incl
    q_decayed = q * jnp.exp(g)[..., None]
    k_to_end = k * jnp.exp(g[..., -1:] - g)[..., None]
    g_end = jnp.exp(g[..., -1])

    def step(state, xs):
        u_c, w_c, qd_c, kend_c, attn_c, gend_c = xs
        v_new = u_c - w_c @ state
        o_c = qd_c @ state + attn_c @ v_new
        state = state * gend_c[..., None, None] + jnp.swapaxes(kend_c, -1, -2) @ v_new
        return state, o_c

    s0 = jnp.zeros((b_, h_, dk, dv), jnp.float32)
    _, o = lax.scan(step, s0, (u, w, q_decayed, k_to_end, attn_intra, g_end))
    o = jnp.moveaxis(o, (0, 2), (1, 3))
    return o.reshape(b_, t_, h_, dv)


def gated_deltanet_mixer(h, w_in, conv_w, a_log, dt_bias, o_norm_w, w_out):
    b_, t_, _ = h.shape
    proj = h @ w_in
    qkv, z, b_raw, a_raw = jnp.split(
        proj, [3 * A_WIDTH, 4 * A_WIDTH, 4 * A_WIDTH + A_HEADS], axis=-1)
    qkv = jax.nn.silu(causal_depthwise_conv(qkv, conv_w))
    q, k, v = jnp.split(qkv, 3, axis=-1)
    hs = (b_, t_, A_HEADS, A_HEAD_DIM)
    q = l2_normalize(q.reshape(hs))
    k = l2_normalize(k.reshape(hs))
    v = v.reshape(hs).astype(jnp.float32)
    beta = jax.nn.sigmoid(b_raw.astype(jnp.float32))
    g_log = -jnp.exp(a_log.astype(jnp.float32)) * jax.nn.softplus(
        a_raw.astype(jnp.float32) + dt_bias.astype(jnp.float32))
    o = chunk_gated_delta_rule(q, k, v, beta, g_log)
    o = rms_norm(o, o_norm_w).astype(h.dtype)
    y = o * jax.nn.silu(z).reshape(hs)
    return y.reshape(b_, t_, A_WIDTH) @ w_out


def forgetting_attention_mixer(h, w_in, f_bias, q_norm_w, k_norm_w, w_out):
    b_, t_, _ = h.shape
    proj = h @ w_in
    q, k, v, z, f_raw = jnp.split(
        proj, [B_WIDTH, 2 * B_WIDTH, 3 * B_WIDTH, 4 * B_WIDTH], axis=-1)
    hs = (b_, t_, B_HEADS, B_HEAD_DIM)
    q = rms_norm(q.reshape(hs), q_norm_w)
    k = rms_norm(k.reshape(hs), k_norm_w)
    v = v.reshape(hs)
    log_f = jax.nn.log_sigmoid(f_raw.astype(jnp.float32) + f_bias.astype(jnp.float32))
    c = jnp.transpose(jnp.cumsum(log_f, axis=1), (0, 2, 1))
    qh = jnp.transpose(q, (0, 2, 1, 3))
    kh = jnp.transpose(k, (0, 2, 1, 3))
    vh = jnp.transpose(v, (0, 2, 1, 3))
    n_blocks = t_ // Q_BLOCK
    q_blocks = jnp.moveaxis(qh.reshape(b_, B_HEADS, n_blocks, Q_BLOCK, B_HEAD_DIM), 2, 0)
    c_blocks = jnp.moveaxis(c.reshape(b_, B_HEADS, n_blocks, Q_BLOCK), 2, 0)
    key_pos = jnp.arange(t_)
    scale = B_HEAD_DIM ** -0.5

    def attend(args):
        qb, cb, blk = args
        s = jnp.einsum("bhqd,bhkd->bhqk", qb, kh).astype(jnp.float32) * scale
        s = s + cb[..., :, None] - c[..., None, :]
        q_pos = blk * Q_BLOCK + jnp.arange(Q_BLOCK)
        mask = key_pos[None, :] <= q_pos[:, None]
        p = jax.nn.softmax(jnp.where(mask, s, -jnp.inf), axis=-1)
        return jnp.einsum("bhqk,bhkd->bhqd", p.astype(vh.dtype), vh)

    o = lax.map(attend, (q_blocks, c_blocks, jnp.arange(n_blocks)))
    o = jnp.moveaxis(o, (0, 2), (1, 3)).reshape(hs)
    y = o * jax.nn.silu(z).reshape(hs)
    return y.reshape(b_, t_, B_WIDTH) @ w_out


def setup_inputs(seed: int = 0) -> dict:
    key = jax.random.key(seed)
    ks = jax.random.split(key, 16)
    f32 = jnp.float32
    x = jax.random.normal(ks[0], (BATCH, SEQ, D_MODEL), f32)
    a_norm_w = 1.0 + 0.02 * jax.random.normal(ks[1], (N_LAYERS_A, D_MODEL), f32)
    a_w_in = jax.random.normal(ks[2], (N_LAYERS_A, D_MODEL, A_IN_COLS), f32) * D_MODEL ** -0.5
    a_conv_w = jax.random.normal(ks[3], (N_LAYERS_A, CONV_WIDTH, 3 * A_WIDTH), f32) * CONV_WIDTH ** -0.5
    a_A_log = jnp.log(jax.random.uniform(ks[4], (N_LAYERS_A, A_HEADS), f32, 1.0, 16.0))
    dt = jnp.exp(jax.random.uniform(ks[5], (N_LAYERS_A, A_HEADS), f32,
                                    math.log(1e-3), math.log(1e-1)))
    a_dt_bias = dt + jnp.log(-jnp.expm1(-dt))
    a_o_norm_w = 1.0 + 0.02 * jax.random.normal(ks[6], (N_LAYERS_A, A_HEAD_DIM), f32)
    a_w_out = jax.random.normal(ks[7], (N_LAYERS_A, A_WIDTH, D_MODEL), f32) * A_WIDTH ** -0.5
    b_norm_w = 1.0 + 0.02 * jax.random.normal(ks[8], (N_LAYERS_B, D_MODEL), f32)
    b_w_in = jax.random.normal(ks[9], (N_LAYERS_B, D_MODEL, B_IN_COLS), f32) * D_MODEL ** -0.5
    b_f_bias = 3.0 + 0.5 * jax.random.normal(ks[10], (N_LAYERS_B, B_HEADS), f32)
    b_q_norm_w = 1.0 + 0.02 * jax.random.normal(ks[11], (N_LAYERS_B, B_HEAD_DIM), f32)
    b_k_norm_w = 1.0 + 0.02 * jax.random.normal(ks[12], (N_LAYERS_B, B_HEAD_DIM), f32)
    b_w_out = jax.random.normal(ks[13], (N_LAYERS_B, B_WIDTH, D_MODEL), f32) * B_WIDTH ** -0.5
    final_norm_w = 1.0 + 0.02 * jax.random.normal(ks[14], (D_MODEL,), f32)
    return {"x": x, "a_norm_w": a_norm_w, "a_w_in": a_w_in, "a_conv_w": a_conv_w,
            "a_A_log": a_A_log, "a_dt_bias": a_dt_bias, "a_o_norm_w": a_o_norm_w,
            "a_w_out": a_w_out, "b_norm_w": b_norm_w, "b_w_in": b_w_in,
            "b_f_bias": b_f_bias, "b_q_norm_w": b_q_norm_w, "b_k_norm_w": b_k_norm_w,
            "b_w_out": b_w_out, "final_norm_w": final_norm_w}


def reference(x, a_norm_w, a_w_in, a_conv_w, a_A_log, a_dt_bias, a_o_norm_w, a_w_out,
              b_norm_w, b_w_in, b_f_bias, b_q_norm_w, b_k_norm_w, b_w_out, final_norm_w):
    h = x
    for i in range(DEPTH):
        j = i // N_MIXERS
        if i % N_MIXERS == 0:
            h = h + gated_deltanet_mixer(rms_norm(h, a_norm_w[j]), a_w_in[j], a_conv_w[j],
                                         a_A_log[j], a_dt_bias[j], a_o_norm_w[j], a_w_out[j])
        else:
            h = h + forgetting_attention_mixer(rms_norm(h, b_norm_w[j]), b_w_in[j], b_f_bias[j],
                                               b_q_norm_w[j], b_k_norm_w[j], b_w_out[j])
    return rms_norm(h, final_norm_w)
```

```python
from contextlib import ExitStack

import numpy as np
import ml_dtypes
import concourse.bass as bass
import concourse.mybir as mybir
from concourse.bass_utils import run_bass_kernel_spmd

F32 = mybir.dt.float32
BF16 = mybir.dt.bfloat16
AF = mybir.ActivationFunctionType
ALU = mybir.AluOpType
AX = mybir.AxisListType

T = 8192
D = 1024
NCORE = 8
EPS = 1e-6
NO_POOL_ALU = True


class Prog:
    ENGS = ("pe", "act", "dve", "pool", "sp")

    def __init__(self, nc):
        self.nc = nc
        self.ops = []

    def op(self, eng, fn, r=(), w=(), dma=None, inc=16):
        self.ops.append(dict(eng=eng, fn=fn, r=tuple(r), w=tuple(w), dma=dma, inc=inc))

    def ext(self, key, sem, value=1):
        self.ops.append(dict(eng="pool", fn=None, r=(), w=(key,), dma=("ext", len(self.ops)), inc=value, ext_sem=sem))

    def emit(self, prefix=""):
        nc = self.nc
        ops = self.ops
        last_w, readers = {}, {}
        for idx, o in enumerate(ops):
            deps = set()
            for k in o["r"]:
                if k in last_w:
                    deps.add(last_w[k])
            for k in o["w"]:
                if k in last_w:
                    deps.add(last_w[k])
                deps.update(readers.get(k, ()))
            for k in o["r"]:
                readers.setdefault(k, []).append(idx)
            for k in o["w"]:
                last_w[k] = idx
                readers[k] = []
            deps.discard(idx)
            eng_dep, dma_dep = {}, {}
            for d in deps:
                od = ops[d]
                if od["dma"] is not None:
                    g = od["dma"]
                    dma_dep[g] = max(dma_dep.get(g, -1), d)
                else:
                    if od["eng"] == "pe" and o["eng"] == "pe" and o["dma"] is None:
                        continue
                    e = od["eng"]
                    eng_dep[e] = max(eng_dep.get(e, -1), d)
            o["edeps"] = eng_dep
            o["ddeps"] = dma_dep
            for d in eng_dep.values():
                ops[d]["sig"] = True
        ecount = {e: 0 for e in self.ENGS}
        dcount = {}
        ext = {}
        for o in ops:
            if o.get("ext_sem") is not None:
                o["cnt"] = 1
                ext[o["dma"]] = o["ext_sem"]
            elif o["dma"] is not None:
                g = o["dma"]
                dcount[g] = dcount.get(g, 0) + 1
                o["cnt"] = dcount[g]
            elif o.get("sig"):
                ecount[o["eng"]] += 1
                o["cnt"] = ecount[o["eng"]]
        esem = {e: nc.alloc_semaphore(prefix + "sem_" + e) for e in self.ENGS}
        dsem = {g: nc.alloc_semaphore(prefix + "dsem_" + str(g)) for g in dcount}
        engobj = {"pe": "tensor", "act": "scalar", "dve": "vector", "pool": "gpsimd", "sp": "sync"}
        self.sems = list(esem.values()) + [h_ for g, h_ in dsem.items() if not str(g).startswith("cc")]
        self.dsem = dict(dsem)
        dsem.update(ext)
        with nc.Block() as block:
            for ename in self.ENGS:
                def body(e, ename=ename):
                    known = {}
                    for o in ops:
                        if o["eng"] != ename:
                            continue
                        for de, d in o["edeps"].items():
                            v = ops[d]["cnt"]
                            if known.get(("e", de), 0) < v:
                                e.wait_ge(esem[de], v)
                                known[("e", de)] = v
                        for g, d in o["ddeps"].items():
                            v = ops[d]["inc"] * ops[d]["cnt"]
                            if known.get(("d", g), 0) < v:
                                e.wait_ge(dsem[g], v)
                                known[("d", g)] = v
                        if o["fn"] is None:
                            continue
                        ins = o["fn"](e)
                        if o["dma"] is not None:
                            ins.then_inc(dsem[o["dma"]], o["inc"])
                        elif o.get("sig"):
                            ins.then_inc(esem[ename], 1)
                getattr(block, engobj[ename])(body)


class PS:
    def __init__(self, ap, bank):
        self.ap, self.bank = ap, bank

    def __getitem__(self, idx):
        return PS(self.ap[idx], self.bank)

    def rearrange(self, *a, **k):
        return PS(self.ap.rearrange(*a, **k), self.bank)


def _u(w, *xs):
    w = list(w)
    outs = []
    for x in xs:
        if isinstance(x, PS):
            k = ("bank", x.bank)
            if k not in w:
                w.append(k)
            outs.append(x.ap)
        else:
            outs.append(x)
    return outs, w


class Stage:
    def __init__(self, nc=None, prefix=""):
        self.nc = nc if nc is not None else bass.Bass("TRN2", target_bir_lowering=False)
        self.P = Prog(self.nc)
        self._n = 0
        self.out_keys = []
        self.prefix = prefix
        self.stack = ExitStack()

    def din(self, name, shape, dt=F32):
        return self.nc.dram_tensor(name, list(shape), dt, kind="ExternalInput").ap()

    def dout(self, name, shape, dt=F32):
        return self.nc.dram_tensor(name, list(shape), dt, kind="ExternalOutput").ap()

    def sb(self, shape, dt=F32, name="t"):
        self._n += 1
        return self.stack.enter_context(self.nc.sbuf_tensor("%s%s_%d" % (self.prefix, name, self._n), list(shape), dt))

    def bank(self, dt=F32):
        self._n += 1
        n = 512 if dt == F32 else 1024
        t = self.stack.enter_context(self.nc.psum_tensor("%sps_%d" % (self.prefix, self._n), [128, n], dt))
        return PS(t, self._n)

    def mm(self, out, lhsT, rhs, r, w, start=True, stop=True):
        (out,), w = _u(w, out)
        self.P.op("pe", lambda e: e.matmul(out, lhsT=lhsT, rhs=rhs, start=start, stop=stop,
                                           skip_group_check=True), r, w)

    def tr(self, out, in_, ident, r, w):
        (out,), w = _u(w, out)
        self.P.op("pe", lambda e: e.transpose(out, in_, ident), r, w)

    def act(self, out, in_, func, r, w, bias=None, scale=None, accum=None):
        kw = {}
        if bias is not None:
            kw["bias"] = bias
        if scale is not None:
            kw["scale"] = scale
        if accum is not None:
            kw["accum_out"] = accum
        (out, in_), w = _u(w, out, in_)
        self.P.op("act", lambda e: e.activation(out=out, in_=in_, func=func, **kw), r, w)

    def copy(self, eng, out, in_, r, w):
        if eng == "act":
            self.act(out, in_, AF.Copy, r, w)
        else:
            (out, in_), w = _u(w, out, in_)
            self.P.op(eng, lambda e: e.tensor_copy(out=out, in_=in_), r, w)

    def tt(self, eng, out, in0, in1, op, r, w):
        (out, in0, in1), w = _u(w, out, in0, in1)
        self.P.op(eng, lambda e: e.tensor_tensor(out=out, in0=in0, in1=in1, op=op), r, w)

    def ts(self, eng, out, in0, s1, s2, op0, op1, r, w):
        if eng == "pool" and NO_POOL_ALU:
            eng = "dve"
        (out, in0), w = _u(w, out, in0)
        if op1 is None:
            self.P.op(eng, lambda e: e.tensor_scalar(out=out, in0=in0, scalar1=s1, scalar2=None, op0=op0), r, w)
        else:
            self.P.op(eng, lambda e: e.tensor_scalar(out=out, in0=in0, scalar1=s1, scalar2=s2, op0=op0, op1=op1), r, w)

    def stt(self, eng, out, in0, scalar, in1, op0, op1, r, w):
        if eng == "pool" and NO_POOL_ALU:
            eng = "dve"
        (out, in0, in1), w = _u(w, out, in0, in1)
        self.P.op(eng, lambda e: e.scalar_tensor_tensor(out=out, in0=in0, scalar=scalar, in1=in1, op0=op0, op1=op1), r, w)

    def recip(self, out, in_, r, w):
        (out, in_), w = _u(w, out, in_)
        self.P.op("dve", lambda e: e.reciprocal(out=out, in_=in_), r, w)

    def memset(self, eng, ap, val, w):
        self.P.op(eng, lambda e: e.memset(ap, val), (), w)

    def load(self, out, in_, w, group, eng="sp", r=()):
        self.P.op(eng, lambda e: e.dma_start(out=out, in_=in_), r, w, dma=group)

    def store(self, out, in_, r, group, eng="sp"):
        key = ("out", group)
        self.P.op(eng, lambda e: e.dma_start(out=out, in_=in_), r, (key,), dma=group)
        if key not in self.out_keys:
            self.out_keys.append(key)

    def finish(self):
        self.P.op("sp", None, self.out_keys, ())
        self.P.emit(self.prefix)
        self.stack.close()
        return self.nc

    def rsqrt(self, out, in_, r, w, scale, bias, tmp):
        self.act(tmp, in_, AF.Ln, r, [("tmp", id(tmp))], bias=bias, scale=scale)
        self.act(out, tmp, AF.Exp, [("tmp", id(tmp))], w, scale=-0.5)

    def load_weight_bf16(self, w_dram, ncols, name, scale_col=None, scale_mul=None):
        wb = self.sb([128, 8, ncols], BF16, name)
        stg = [self.sb([128, ncols], F32, name + "_stg") for _ in range(2)]
        wv = w_dram.rearrange("(fc p) c -> p fc c", p=128)
        for fc in range(8):
            s = stg[fc % 2]
            sk = (name + "_stg", fc % 2)
            self.load(s[:], wv[:, fc, :], [sk], name + "_ld%d" % (fc % 2))
            eng = "dve" if fc % 2 == 0 else "pool"
            if scale_col is None:
                self.copy(eng, wb[:, fc, :], s[:], [sk], [(name, fc)])
            else:
                self.ts(eng, wb[:, fc, :], s[:], scale_col[:, fc:fc + 1], scale_mul, ALU.mult, ALU.mult,
                        [sk, "normw"], [(name, fc)])
        return wb, [(name, fc) for fc in range(8)]


RG = [[0, 1, 2, 3], [4, 5, 6, 7]]
NCHK = 4


class Chunked:
    def __init__(self, nc, name, rows, dt):
        self.w = T // NCHK
        self.t = [nc.dram_tensor("%s%d" % (name, k), [rows, self.w], dt).ap() for k in range(NCHK)]

    def tile(self, s):
        k = (s * 512) // self.w
        off = s * 512 - k * self.w
        return self.t[k][:, off:off + 512], k


def allgather_chunked(S, src, dst):
    for k in range(NCHK):
        allgather(S, src.t[k], dst.t[k], ("ag", k), "cc%d" % k)


def allgather(S, src, dst, key, group):
    S.P.op("pool", lambda e: e.collective_compute("AllGather", ALU.bypass, replica_groups=RG,
                                                  ins=[src.opt()], outs=[dst.opt()]),
           (), [key], dma=group, inc=1)


def build_rows_proj(S, yall, resrows, wcols, hrows, ss_src, in_key=()):
    wb, wkeys = S.load_weight_bf16(wcols, 256, "wo")
    ones = S.sb([128, 128], BF16, "ones")
    S.memset("pool", ones[:], 1.0, ["ones"])
    rv = resrows.rearrange("(oc p) t -> p oc t", p=128)
    hv = hrows.rearrange("(oc p) t -> p oc t", p=128)
    yt = [S.sb([128, 8, 512], BF16, "yt") for _ in range(2)]
    rt = [S.sb([128, 2, 512], F32, "rt") for _ in range(2)]
    h = [S.sb([128, 2, 512], F32, "h") for _ in range(2)]
    sq = S.sb([128, 2, 512], BF16, "sq")
    ssrow = S.sb([1, T], F32, "ssrow")
    pb = [S.bank(), S.bank()]
    ssb = S.bank()
    NSUP = T // 512

    def ld(s):
        sl = s % 2
        yap, k = yall.tile(s)
        S.load(yt[sl][:], yap.rearrange("(fc p) t -> p fc t", p=128), [("yt", sl)], "ld_y%d" % sl, r=[("ag", k)])
        S.load(rt[sl][:], rv[:, :, s * 512:(s + 1) * 512], [("rt", sl)], "ld_r%d" % sl)

    ld(0)
    for s in range(NSUP):
        sl = s % 2
        if s + 1 < NSUP:
            ld(s + 1)
        for oc in range(2):
            for fc in range(8):
                S.mm(pb[oc][:], wb[:, fc, oc * 128:(oc + 1) * 128], yt[sl][:, fc, :], [wkeys[fc], ("yt", sl)],
                     [("pb", oc)], start=(fc == 0), stop=(fc == 7))
            S.tt("dve", h[sl][:, oc, :], pb[oc][:], rt[sl][:, oc, :], ALU.add, [("pb", oc), ("rt", sl)], [("h", sl, oc)])
            S.act(sq[:, oc, :], h[sl][:, oc, :], AF.Square, [("h", sl, oc)], [("sq", oc)])
        for oc in range(2):
            S.mm(ssb[:], ones[:], sq[:, oc, :], ["ones", ("sq", oc)], ["ssb"], start=(oc == 0), stop=(oc == 1))
        S.copy("act", ssrow[0:1, s * 512:(s + 1) * 512], ssb[0:1, :], ["ssb"], [("ssrow", s)])
        S.store(hv[:, :, s * 512:(s + 1) * 512], h[sl][:], [("h", sl, 0), ("h", sl, 1)], "st_h%d" % sl)
    S.store(ss_src, ssrow[:], [("ssrow", s) for s in range(NSUP)], "st_ss")


def build_rows_norm(S, hrows, ss_all, normw2, dst, dst_dt, in_key=()):
    nw = S.sb([128, 2], F32, "nw")
    S.load(nw[:], normw2, ["nw"], "ld_nw")
    onesf = S.sb([128, 128], F32, "onesf")
    S.memset("pool", onesf[:], 1.0, ["onesf"])
    hv = hrows.rearrange("(oc p) t -> p oc t", p=128)
    ssz = [S.sb([128, 512], F32, "ssz") for _ in range(2)]
    for z in range(2):
        S.memset("pool", ssz[z][:], 0.0, [("ssz", z)])
    h = [S.sb([128, 2, 512], F32, "h") for _ in range(2)]
    xo = [S.sb([128, 2, 512], dst_dt, "xo") for _ in range(2)]
    lt = S.sb([128, 512], F32, "lt")
    rstd = S.sb([128, 512], F32, "rstd")
    tb = S.bank()
    NSUP = T // 512

    def ld(s):
        sl = s % 2
        S.load(ssz[sl][0:4, :], ss_all[:, s * 512:(s + 1) * 512], [("ssz", sl)], "ld_s%d" % sl, r=in_key)
        S.load(h[sl][:], hv[:, :, s * 512:(s + 1) * 512], [("h", sl)], "ld_h%d" % sl)

    ld(0)
    for s in range(NSUP):
        sl = s % 2
        if s + 1 < NSUP:
            ld(s + 1)
        S.mm(tb[:], onesf[:], ssz[sl][:], ["onesf", ("ssz", sl)], ["tb"])
        S.rsqrt(rstd[:], tb[:], ["tb"], ["rstd"], 1.0 / D, EPS, lt[:])
        for oc in range(2):
            S.stt("dve", xo[sl][:, oc, :], h[sl][:, oc, :], nw[:, oc:oc + 1], rstd[:], ALU.mult, ALU.mult,
                  [("h", sl), "rstd", "nw"], [("xo", sl, oc)])
        dap = dst.tile(s)[0] if isinstance(dst, Chunked) else dst[:, s * 512:(s + 1) * 512]
        S.store(dap.rearrange("(oc p) t -> p oc t", p=128), xo[sl][:], [("xo", sl, 0), ("xo", sl, 1)], "st_x%d" % sl)


def build_fox_stage(S, xnT, win, fbias, qkw, ident_d, ut_d, y1T, in_key=(), ag_dst=None):

    NT = T // 128
    NQB = T // 512
    scale = 128.0 ** -0.5

    wb, wkeys = S.load_weight_bf16(win, 1026, "wi")
    identf = S.sb([128, 128], F32, "identf")
    identb = S.sb([128, 128], BF16, "identb")
    ut = S.sb([128, 128], F32, "ut")
    onesf = S.sb([128, 128], F32, "onesf")
    maskT = S.sb([128, 128], BF16, "maskT")
    fb = S.sb([128, 2], F32, "fb")
    nwq = S.sb([128, 512], F32, "nwq")
    S.load(identf[:], ident_d, ["identf"], "ld_c0")
    S.load(ut[:], ut_d, ["ut"], "ld_c1")
    S.load(fb[:], fbias.partition_broadcast(128), ["fb"], "ld_c2")
    S.load(nwq[:], qkw.partition_broadcast(128), ["nwq"], "ld_c3")
    S.copy("dve", identb[:], identf[:], ["identf"], ["identb"])
    S.copy("dve", maskT[:], ut[:], ["ut"], ["maskT"])
    S.memset("pool", onesf[:], 1.0, ["onesf"])

    QKT = S.sb([128, 4, T], BF16, "QKT")
    V = S.sb([128, NT, 2, 129], BF16, "V")
    G = S.sb([128, NT, 256], BF16, "G")
    Fraw = S.sb([128, NT, 2], F32, "Fraw")
    S.memset("pool", V[:, :, :, 128:129], 1.0, ["Vones"])

    xb = [S.sb([128, 8, 512], BF16, "xb") for _ in range(2)]
    sqk = S.sb([128, 512], F32, "sqk")
    ss4 = S.sb([128, 4], F32, "ss4")
    l4 = S.sb([128, 4], F32, "l4")
    r4 = S.sb([128, 4], F32, "r4")
    qkn = [S.sb([128, 4, 128], BF16, "qkn") for _ in range(2)]

    banks = [S.bank() for _ in range(6)]
    tpbs = [S.bank(BF16), S.bank(BF16)]
    tpb = tpbs[0]

    def ldx(s):
        xap, k = xnT.tile(s)
        S.load(xb[s % 2][:], xap.rearrange("(fc p) t -> p fc t", p=128), [("xb", s % 2)], "ld_x%d" % (s % 2),
               r=[("ag", k)])

    def emit_tr(t):
        sl = t % 2
        for seg in range(4):
            S.tr(tpbs[sl][:, seg * 128:(seg + 1) * 128], qkn[sl][:, seg, :], identb[:],
                 [("qkn", sl, seg), "identb"], [("tp", sl)])
        S.copy("dve", QKT[:, :, t * 128:(t + 1) * 128],
               tpbs[sl][:, 0:512].rearrange("p (a b) -> p a b", a=4), [("tp", sl)], [("QKT", t)])

    pending = []
    ldx(0)
    for s in range(T // 512):
        if s + 1 < T // 512:
            ldx(s + 1)
        for j in range(4):
            t = 4 * s + j
            sl = t % 2
            qk_ps, vz_ps, f_ps = banks[sl], banks[2 + sl], banks[4 + sl]
            xk = ("xb", s % 2)
            for fc in range(8):
                lhs = xb[s % 2][:, fc, j * 128:(j + 1) * 128]
                S.mm(qk_ps[:], lhs, wb[:, fc, 0:512], [xk, wkeys[fc]], [("qk", sl)], start=(fc == 0), stop=(fc == 7))
            for fc in range(8):
                lhs = xb[s % 2][:, fc, j * 128:(j + 1) * 128]
                S.mm(vz_ps[:], lhs, wb[:, fc, 512:1024], [xk, wkeys[fc]], [("vz", sl)], start=(fc == 0), stop=(fc == 7))
            for fc in range(8):
                lhs = xb[s % 2][:, fc, j * 128:(j + 1) * 128]
                S.mm(f_ps[:, 0:2], lhs, wb[:, fc, 1024:1026], [xk, wkeys[fc]], [("fp", sl)], start=(fc == 0), stop=(fc == 7))
            S.act(sqk[:], qk_ps[:], AF.Square, [("qk", sl)], ["sqk"])
            S.P.op("dve", lambda e: e.reduce_sum(out=ss4[:], in_=sqk[:].rearrange("p (a b) -> p a b", a=4), axis=AX.X),
                   ["sqk"], ["ss4"])
            S.rsqrt(r4[:], ss4[:], ["ss4"], ["r4"], 1.0 / 128, EPS, l4[:])
            for seg in range(4):
                S.stt("dve", qkn[sl][:, seg, :], qk_ps[:, seg * 128:(seg + 1) * 128], r4[:, seg:seg + 1],
                      nwq[:, seg * 128:(seg + 1) * 128], ALU.mult, ALU.mult,
                      [("qk", sl), "r4", "nwq"], [("qkn", sl, seg)])
            S.copy("act", V[:, t, :, 0:128], vz_ps[:, 0:256].rearrange("p (a b) -> p a b", a=2), [("vz", sl)], [("V", t)])
            S.copy("act", G[:, t, :], vz_ps[:, 256:512], [("vz", sl)], [("G", t)])
            S.copy("dve", Fraw[:, t, :], f_ps[:, 0:2], [("fp", sl)], [("Fraw", t)])
            pending.append(t)
            if len(pending) > 1:
                emit_tr(pending.pop(0))
    while pending:
        emit_tr(pending.pop(0))

    for t8 in range(NT // 8):
        ks = [("G", t) for t in range(t8 * 8, t8 * 8 + 8)]
        S.act(G[:, t8 * 8:(t8 + 1) * 8, :], G[:, t8 * 8:(t8 + 1) * 8, :], AF.Silu, ks, ks)
    fk = [("Fraw", t) for t in range(NT)]
    xf = S.sb([128, NT, 2], F32, "xf")
    ab = S.sb([128, NT, 2], F32, "ab")
    L = S.sb([128, NT, 2], F32, "L")
    S.tt("dve", xf[:], Fraw[:], fb[:].unsqueeze(1).to_broadcast([128, NT, 2]), ALU.add, fk + ["fb"], ["xf"])
    S.act(ab[:], xf[:], AF.Abs, ["xf"], ["ab"])
    S.act(ab[:], ab[:], AF.Exp, ["ab"], ["ab"], scale=-1.0)
    S.act(ab[:], ab[:], AF.Ln, ["ab"], ["ab"], bias=1.0)
    S.ts("dve", xf[:], xf[:], 0.0, None, ALU.min, None, ["xf"], ["xf"])
    S.tt("dve", L[:], xf[:], ab[:], ALU.subtract, ["xf", "ab"], ["L"])
    cs_ps, tot_ps = banks[0], banks[1]
    Lf = L[:].rearrange("p a b -> p (a b)")
    S.mm(cs_ps[:, 0:128], ut[:], Lf, ["ut", "L"], [("qk", 0)])
    S.mm(tot_ps[:, 0:128], onesf[:], Lf, ["onesf", "L"], [("qk", 1)])
    tot = S.sb([128, NT, 2], F32, "tot")
    pa = S.sb([128, NT, 2], F32, "pa")
    pbuf = S.sb([128, NT, 2], F32, "pbuf")
    S.copy("dve", tot[:], tot_ps[:, 0:128].rearrange("p (a b) -> p a b", b=2), [("qk", 1)], ["tot"])
    S.copy("dve", pa[:], tot[:], ["tot"], ["pa"])
    cur, nxt, ck, nk = pa, pbuf, "pa", "pbuf"
    sh = 1
    while sh < NT:
        S.copy("dve", nxt[:, 0:sh, :], cur[:, 0:sh, :], [ck], [nk])
        S.tt("dve", nxt[:, sh:NT, :], cur[:, sh:NT, :], cur[:, 0:NT - sh, :], ALU.add, [ck], [nk])
        cur, nxt, ck, nk = nxt, cur, nk, ck
        sh *= 2
    incl, inclk = cur, ck
    c = S.sb([128, NT, 2], F32, "c")
    S.tt("dve", c[:], incl[:], tot[:], ALU.subtract, [inclk, "tot"], ["c0"])
    S.tt("dve", c[:], c[:], cs_ps[:, 0:128].rearrange("p (a b) -> p a b", b=2), ALU.add, ["c0", ("qk", 0)], ["c"])
    gam = S.sb([128, NT, 2], F32, "gam")
    i4 = incl[:].rearrange("p (a b) h -> p a b h", b=4)
    S.tt("dve", gam[:].rearrange("p (a b) h -> p a b h", b=4), i4, i4[:, :, 0:1, :].to_broadcast([128, NQB, 4, 2]),
         ALU.subtract, [inclk], ["gam0"])
    S.act(gam[:], gam[:], AF.Exp, ["gam0"], ["gam"])

    biasP = [S.sb([128, NT], F32, "biasP") for _ in range(2)]
    biasD = [S.sb([128, 4, 4], F32, "biasD") for _ in range(2)]
    NPT = 4
    PT = [S.sb([128, 512], BF16, "PT") for _ in range(NPT)]
    od = [S.sb([128, 129], F32, "od") for _ in range(4)]
    osb = [S.sb([128, 129], F32, "osb") for _ in range(4)]
    rcp = [S.sb([128, 1], F32, "rcp") for _ in range(4)]
    ysb = [S.sb([128, 128], BF16, "ysb") for _ in range(4)]
    ybuf = [S.sb([128, 512], BF16, "ybuf") for _ in range(2)]
    st_ps = [banks[0], banks[1]]
    accP = [banks[2][:, 0:129], banks[2][:, 256:385], banks[3][:, 0:129], banks[3][:, 256:385]]
    accD = [banks[4][:, 0:129], banks[4][:, 256:385], banks[5][:, 0:129], banks[5][:, 256:385]]
    groups = [(i, hh) for i in range(NQB) for hh in range(2)]
    tiles = []
    for gi, (i, hh) in enumerate(groups):
        seq = [("past", j) for j in range(4 * i)] + [("diag", jp) for jp in range(4)]
        for q, (kind, idx) in enumerate(seq):
            tiles.append((gi, kind, idx, q == 0, q == len(seq) - 1))

    def setup(gi):
        i, hh = groups[gi]
        bs = gi % 2
        npast = 4 * i
        if npast:
            S.ts("dve", biasP[bs][:, 0:npast], c[:, 0:npast, hh], -1.0, incl[:, 4 * i, hh:hh + 1], ALU.mult, ALU.add,
                 ["c", inclk], [("biasP", bs)])
        for u in range(4):
            S.ts("dve", biasD[bs][:, u, 0:u + 1], c[:, 4 * i:4 * i + u + 1, hh], -1.0,
                 incl[:, 4 * i + u, hh:hh + 1], ALU.mult, ALU.add, ["c", inclk], [("biasD", bs)])

    def emit_qk(k):
        gi, kind, idx, _, _ = tiles[k]
        i, hh = groups[gi]
        sb_ = k % 2
        qkeys = [("QKT", 4 * i + u) for u in range(4)]
        if kind == "past":
            S.mm(st_ps[sb_][:], QKT[:, 2 + hh, idx * 128:(idx + 1) * 128], QKT[:, hh, i * 512:(i + 1) * 512],
                 [("QKT", idx)] + qkeys, [("qk", sb_)])
        else:
            kt = 4 * i + idx
            w = (4 - idx) * 128
            S.mm(st_ps[sb_][:, 0:w], QKT[:, 2 + hh, kt * 128:(kt + 1) * 128],
                 QKT[:, hh, i * 512 + idx * 128:(i + 1) * 512], [("QKT", kt)] + qkeys, [("qk", sb_)])

    def emit_rest(k):
        gi, kind, idx, _, _ = tiles[k]
        i, hh = groups[gi]
        bs = gi % 2
        sb_ = k % 2
        pb_ = k % NPT
        npast = 4 * i
        if kind == "past":
            j = idx
            S.act(PT[pb_][:], st_ps[sb_][:], AF.Exp, [("qk", sb_), ("biasP", bs)],
                  [("PTd", pb_, u) for u in range(4)], bias=biasP[bs][:, j:j + 1], scale=scale)
            for u in range(4):
                S.mm(accP[u], PT[pb_][:, u * 128:(u + 1) * 128], V[:, j, hh, :], [("PTd", pb_, u), ("V", j), "Vones"],
                     [("accP", u)], start=(j == 0 and u % 2 == 0), stop=(j == npast - 1))
        else:
            jp = idx
            kt = 4 * i + jp
            for u in range(jp, 4):
                pk = ("PTd", pb_, u)
                S.act(PT[pb_][:, u * 128:(u + 1) * 128], st_ps[sb_][:, (u - jp) * 128:(u - jp + 1) * 128], AF.Exp,
                      [("qk", sb_), ("biasD", bs)], [pk], bias=biasD[bs][:, u, jp:jp + 1], scale=scale)
                if u == jp:
                    S.tt("pool", PT[pb_][:, u * 128:(u + 1) * 128], PT[pb_][:, u * 128:(u + 1) * 128], maskT[:],
                         ALU.mult, [pk, "maskT"], [pk])
                S.mm(accD[u], PT[pb_][:, u * 128:(u + 1) * 128], V[:, kt, hh, :], [pk, ("V", kt), "Vones"],
                     [("accD", u)], start=(jp == 0 and u % 2 == 0), stop=(jp == u))

    def emit_final(gi):
        i, hh = groups[gi]
        bs = gi % 2
        npast = 4 * i
        for u in range(4):
            if npast:
                S.copy("act", od[u][:], accD[u], [("accD", u)], [("od", u)])
            else:
                S.copy("act", osb[u][:], accD[u], [("accD", u)], [("osb", u)])
        if npast:
            for u in range(4):
                S.stt("dve", osb[u][:], accP[u], gam[:, 4 * i + u, hh:hh + 1], od[u][:], ALU.mult, ALU.add,
                      [("accP", u), "gam", ("od", u)], [("osb", u)])
        for u in range(4):
            S.recip(rcp[u][:], osb[u][:, 128:129], [("osb", u)], [("rcp", u)])
        for u in range(4):
            tq = 4 * i + u
            S.stt("dve", ysb[u][:], osb[u][:, 0:128], rcp[u][:, 0:1], G[:, tq, hh * 128:(hh + 1) * 128], ALU.mult, ALU.mult,
                  [("osb", u), ("rcp", u), ("G", tq)], [("ysb", u)])
        for u in range(4):
            ys = u % 2
            S.tr(tpbs[ys][:, (u // 2) * 128:(u // 2 + 1) * 128], ysb[u][:], identb[:], [("ysb", u), "identb"], [("ty", u)])
        for u in range(4):
            ys = u % 2
            S.copy("act", ybuf[bs][:, u * 128:(u + 1) * 128], tpbs[ys][:, (u // 2) * 128:(u // 2 + 1) * 128], [("ty", u)],
                   [("ybuf", bs, u)])
        S.store(y1T.tile(i)[0][hh * 128:(hh + 1) * 128, :], ybuf[bs][:],
                [("ybuf", bs, u) for u in range(4)], "st_y%d" % bs)

    def issue_ag(kc):
        S.P.op("pool", lambda e: e.collective_compute("AllGather", ALU.bypass, replica_groups=RG,
                                                      ins=[y1T.t[kc].opt()], outs=[ag_dst.t[kc].opt()]),
               [("out", "st_y0"), ("out", "st_y1")], [("agout", kc)], dma="cc%d" % kc, inc=1)

    NTL = len(tiles)
    setup(0)
    emit_qk(0)
    ag_wait = []
    for k in range(NTL):
        if k + 1 < NTL:
            if tiles[k + 1][3]:
                setup(tiles[k + 1][0])
            emit_qk(k + 1)
        emit_rest(k)
        if tiles[k][4]:
            gi = tiles[k][0]
            emit_final(gi)
            i_, hh_ = groups[gi]
            if ag_dst is not None and hh_ == 1 and i_ % 4 == 3:
                ag_wait.append([i_ // 4, 6])
        for aw in list(ag_wait):
            aw[1] -= 1
            if aw[1] <= 0 or k == NTL - 1:
                issue_ag(aw[0])
                ag_wait.remove(aw)
    if ag_dst is not None:
        S.P.op("pool", None, [("agout", k) for k in range(NCHK)], ())
    return S.finish()


_CACHE = {}


def _get(name, fn):
    if name not in _CACHE:
        _CACHE[name] = fn()
    return _CACHE[name]


def _consts():
    ident = np.eye(128, dtype=np.float32)
    k = np.arange(128)
    ut = (k[:, None] <= k[None, :]).astype(np.float32)
    return ident, ut


def _run(nc, in_maps):
    res = run_bass_kernel_spmd(nc, in_maps, core_ids=list(range(NCORE)))
    return res.results


def fox_inputs(xnT_b, b_w_in, b_f_bias, b_q_norm_w, b_k_norm_w, g):
    h0, h1 = 2 * g, 2 * g + 1
    cols = []
    for base in (0, 1024, 2048, 3072):
        for h in (h0, h1):
            cols.extend(range(base + h * 128, base + (h + 1) * 128))
    cols.extend([4096 + h0, 4096 + h1])
    ident, ut = _consts()
    return {
        "win": np.ascontiguousarray(b_w_in[0][:, cols]),
        "fbias": np.ascontiguousarray(b_f_bias[0][[h0, h1]].reshape(1, 2)),
        "qkw": np.ascontiguousarray(np.concatenate([b_q_norm_w[0], b_q_norm_w[0], b_k_norm_w[0], b_k_norm_w[0]]).reshape(1, 512)),
        "ident": ident, "ut": ut,
    }


BIG = 30000.0


class _Stop(Exception):
    pass


def build_gdn_stage(S, xT, win, normw, convw, hconst, onw_d, cm, sw_d, y0T, nsup=T // 512, dbg=99, ag_dst=None):

    def ck(level):
        if dbg < level:
            raise _Stop()

    nw = S.sb([128, 8], F32, "nw")
    S.load(nw[:], normw, ["normw"], "ld_nw")
    wb, wkeys = S.load_weight_bf16(win, 1028, "wi", scale_col=nw, scale_mul=1.0)
    cmat = S.sb([128, 7, 128], F32, "cmat")
    S.load(cmat[:], cm.rearrange("a p f -> p a f"), ["cmat"], "ld_cm")
    identf, maskS, maskTn, LTbd, ONESbd, SEL0, SEL1 = [cmat[:, a, :] for a in range(7)]
    swm = S.sb([128, 128], F32, "swm")
    S.load(swm[:], sw_d, ["swm"], "ld_sw")
    cw = S.sb([128, 24], F32, "cw")
    S.load(cw[:], convw, ["cw"], "ld_cw")
    hc = S.sb([128, 2], F32, "hc")
    S.load(hc[:], hconst, ["hc"], "ld_hc")
    onwb = S.sb([128, 128], F32, "onwb")
    S.load(onwb[:], onw_d.partition_broadcast(128), ["onwb"], "ld_onw")
    identb = S.sb([128, 128], BF16, "identb")
    S.copy("dve", identb[:], identf, ["cmat"], ["identb"])
    onesb = S.sb([128, 128], BF16, "onesb")
    S.memset("pool", onesb[:], 1.0, ["onesb"])
    onesf = S.sb([128, 128], F32, "onesf")
    S.memset("pool", onesf[:], 1.0, ["onesf"])
    DW = S.sb([128, 24, 128], BF16, "DW")
    for ct in range(24):
        S.ts("dve" if ct % 2 else "pool", DW[:, ct, :], identf, cw[:, ct:ct + 1], None, ALU.mult, None, ["cmat", "cw"], ["DW"])
    nexpA = S.sb([128, 1], F32, "nexpA")
    S.act(nexpA[:], hc[:, 0:1], AF.Exp, ["hc"], ["nexpA0"])
    S.ts("dve", nexpA[:], nexpA[:], -1.0, None, ALU.mult, None, ["nexpA0"], ["nexpA"])

    xv = xT.rearrange("(fc p) t -> p fc t", p=128)
    xs = [S.sb([128, 8, 512], F32, "xs") for _ in range(2)]
    xb = [S.sb([128, 8, 512], BF16, "xb") for _ in range(2)]
    xsq = [S.sb([128, 8, 512], BF16, "xsq") for _ in range(2)]
    lt = S.sb([128, 512], F32, "lt")
    rstd = [S.sb([128, 512], F32, "rstd") for _ in range(2)]
    prebf = [S.sb([128, 515], BF16, "prebf") for _ in range(6)]
    for cc in range(6):
        S.memset("pool", prebf[cc][:], 0.0, [("prebf", cc)])
    qks = [S.sb([128, 512], F32, "qks") for _ in range(4)]
    sq2 = [S.sb([128, 512], BF16, "sq2") for _ in range(4)]
    lt2 = S.sb([128, 512], F32, "lt2")
    r2 = [S.sb([128, 512], F32, "r2") for _ in range(2)]
    zp = [S.sb([128, 512], F32, "zp") for _ in range(2)]
    fm = {nm: [S.sb([128, 8, 2, 64], BF16, nm) for _ in range(2)] for nm in ("qT2", "kT2", "vT2", "zT2")}
    ybuf = [S.sb([128, 2, 8, 64], BF16, "ybuf") for _ in range(2)]
    BG = S.sb([128, 8, 2], F32, "BG")
    tmsb = S.sb([128, 4], F32, "tmsb")
    rcol = S.sb([128, 1], F32, "rcol")
    lcol = S.sb([128, 1], F32, "lcol")
    sc = {nm: S.sb([128, 8], F32, nm) for nm in ("beta", "nbeta", "xa", "ab", "sp", "glog", "eg", "beg", "dd", "ek")}
    gs = S.sb([128, 4, 8], F32, "gs")
    ge = S.sb([128, 2, 8], F32, "ge")

    FA = [S.bank(), S.bank()]
    FB = [S.bank(), S.bank()]
    H = [S.bank(BF16), S.bank(BF16)]
    FS = S.bank()
    HS = S.bank(BF16)
    pre_ps = [FA[0], FA[1]]
    cv_ps = FB[0]
    ss_ps = FB[1]
    sm_ps = FS[:, 0:128]

    NPS = 4

    def mk(nm, dt=BF16, shape=(128, 128)):
        return S.sb(list(shape), dt, nm)
    W = []
    for q in range(2):
        W.append(dict(kbeg=mk("kbeg"), dg=mk("dg", F32), a1=mk("a1", F32), a2=mk("a2", F32), Ds=mk("Ds", F32),
                      DTi=mk("DTi", F32), Erow=mk("Erow", F32), X=mk("X"), XT=mk("XT"),
                      Pb=[mk("Pb"), mk("Pb")], Yb=[mk("Yb"), mk("Yb")], TTb=[mk("TTb"), mk("TTb")]))
    R = []
    for p_ in range(NPS):
        d_ = dict(TT=mk("TT"), vb=mk("vb"), nw0=mk("nw0"), nw1=mk("nw1"), qd0=mk("qd0"), qd1=mk("qd1"),
                  attnT=mk("attnT"), kend0=mk("kend0"), kend1=mk("kend1"), gw=mk("gw", F32))
        for nm in ("nw0", "nw1", "qd0", "qd1", "kend0", "kend1"):
            S.memset("pool", d_[nm][:], 0.0, [(nm, p_)])
        R.append(d_)
    vnb = mk("vnb")
    Sf = mk("Sf", F32, (128, 256))
    Sbf = mk("Sbf", BF16, (128, 256))
    S.memset("pool", Sf[:], 0.0, ["Sf"])
    S.memset("pool", Sbf[:], 0.0, ["Sbf"])
    osq = mk("osq", F32)
    oss = S.sb([128, 1], F32, "oss")
    ol = S.sb([128, 1], F32, "ol")
    orstd = S.sb([128, 1], F32, "orstd")
    ysb = mk("ysb")

    def ldx(s):
        S.load(xs[s % 2][:], xv[:, :, s * 512:(s + 1) * 512], [("xs", s % 2)], "ld_x%d" % (s % 2))

    def front(s):
        sl = s % 2
        xsk = ("xs", sl)
        S.copy("pool", xb[sl][:], xs[sl][:], [xsk], [("xb", sl)])
        S.act(xsq[sl][:], xs[sl][:], AF.Square, [xsk], [("xsq", sl)])
        for fc in range(8):
            S.mm(ss_ps[:], onesb[:], xsq[sl][:, fc, :], ["onesb", ("xsq", sl)], ["ss"], start=(fc == 0), stop=(fc == 7))
        S.rsqrt(rstd[sl][:], ss_ps[:], ["ss"], [("rstd", sl)], 1.0 / D, EPS, lt[:])

    def st0(s, cc):
        sl = s % 2
        p = pre_ps[cc % 2]
        pk = ("pre", cc % 2)
        for fc in range(8):
            S.mm(p[:], wb[:, fc, cc * 128:(cc + 1) * 128], xb[sl][:, fc, :], [wkeys[fc], ("xb", sl)], [pk],
                 start=(fc == 0), stop=(fc == 7))
        if cc < 6:
            S.tt("dve", prebf[cc][:, 3:515], p[:], rstd[sl][:], ALU.mult, [pk, ("rstd", sl)], [("prebf", cc)])
        else:
            S.tt("dve", zp[cc % 2][:], p[:], rstd[sl][:], ALU.mult, [pk, ("rstd", sl)], [("zp", cc % 2)])

    def st1(s, cc):
        sl = s % 2
        h = cc % 2
        typ = cc // 2
        b2 = cc % 2
        if cc < 6:
            bk = ("prebf", cc)
            for tap in range(4):
                S.mm(cv_ps[:], DW[:, cc * 4 + tap, :], prebf[cc][:, tap:tap + 512], ["DW", bk], ["cv"],
                     start=(tap == 0), stop=(tap == 3))
            S.copy("pool", prebf[cc][:, 0:3], prebf[cc][:, 512:515], [bk], [bk])
            if typ == 2:
                S.act(fm["vT2"][sl][:, :, h, :], cv_ps[:].rearrange("p (n i) -> p n i", n=8), AF.Silu,
                      ["cv"], [("vT2", sl, h)])
            else:
                S.act(qks[cc][:], cv_ps[:], AF.Silu, ["cv"], [("qks", cc)])
                S.act(sq2[cc][:], qks[cc][:], AF.Square, [("qks", cc)], [("sq2", cc)])
        else:
            S.act(fm["zT2"][sl][:, :, h, :], zp[b2][:].rearrange("p (n i) -> p n i", n=8), AF.Silu,
                  [("zp", b2)], [("zT2", sl, h)])

    def st2(s, cc):
        sl = s % 2
        h = cc % 2
        typ = cc // 2
        b2 = cc % 2
        if typ > 1:
            return
        S.mm(ss_ps[:], onesb[:], sq2[cc][:], ["onesb", ("sq2", cc)], ["ss"])
        S.rsqrt(r2[b2][:], ss_ps[:], ["ss"], [("r2", b2)], 1.0, EPS, lt2[:])
        nm = "qT2" if typ == 0 else "kT2"
        q3 = qks[cc][:].rearrange("p (n i) -> p n i", n=8)
        r3 = r2[b2][:].rearrange("p (n i) -> p n i", n=8)
        if typ == 0:
            S.stt("dve", fm[nm][sl][:, :, h, :], q3, 128.0 ** -0.5, r3, ALU.mult, ALU.mult,
                  [("qks", cc), ("r2", b2)], [(nm, sl, h)])
        else:
            S.tt("dve", fm[nm][sl][:, :, h, :], q3, r3, ALU.mult, [("qks", cc), ("r2", b2)], [(nm, sl, h)])

    def tmstep(s, jj):
        sl = s % 2
        tm_ps = sm_ps[:, 0:5]
        sw_ps = sm_ps[:, 8:12]
        for fc in range(8):
            S.mm(tm_ps[:, 0:4], xb[sl][:, fc, jj * 128:(jj + 1) * 128], wb[:, fc, 1024:1028], [("xb", sl), wkeys[fc]], ["tm"],
                 start=(fc == 0), stop=(fc == 7))
        for fc in range(8):
            S.mm(tm_ps[:, 4:5], xsq[sl][:, fc, jj * 128:(jj + 1) * 128], onesb[:, 0:1], [("xsq", sl), "onesb"], ["tmss"],
                 start=(fc == 0), stop=(fc == 7))
        S.rsqrt(rcol[:], tm_ps[:, 4:5], ["tmss"], ["rcol"], 1.0 / D, EPS, lcol[:])
        S.ts("dve", tmsb[:], tm_ps[:, 0:4], rcol[:, 0:1], None, ALU.mult, None, ["tm", "rcol"], ["tmsb"])
        S.mm(sw_ps, swm[:], tmsb[:], ["swm", "tmsb"], ["sw"])
        n0, n1 = 2 * jj, 2 * jj + 1
        S.copy("pool", BG[0:64, n0, :], tmsb[0:64, 0:2], ["tmsb"], [("BG", jj, 0)])
        S.copy("dve", BG[64:128, n0, :], sw_ps[64:128, 2:4], ["sw"], [("BG", jj, 1)])
        S.copy("dve", BG[0:64, n1, :], sw_ps[0:64, 0:2], ["sw"], [("BG", jj, 2)])
        S.copy("pool", BG[64:128, n1, :], tmsb[64:128, 2:4], ["tmsb"], [("BG", jj, 3)])

    def scalars(s):
        sl = s % 2
        bgk = [("BG", jj, q) for jj in range(4) for q in range(4)]
        S.act(sc["beta"][:], BG[:, :, 0], AF.Exp, bgk, ["beta0"], scale=-1.0)
        S.ts("dve", sc["beta"][:], sc["beta"][:], 1.0, None, ALU.add, None, ["beta0"], ["beta1"])
        S.recip(sc["beta"][:], sc["beta"][:], ["beta1"], ["beta"])
        S.ts("dve", sc["nbeta"][:], sc["beta"][:], -1.0, None, ALU.mult, None, ["beta"], ["nbeta"])
        S.ts("dve", sc["xa"][:], BG[:, :, 1], hc[:, 1:2], None, ALU.add, None, bgk + ["hc"], ["xa"])
        S.act(sc["ab"][:], sc["xa"][:], AF.Abs, ["xa"], ["ab0"])
        S.act(sc["ab"][:], sc["ab"][:], AF.Exp, ["ab0"], ["ab1"], scale=-1.0)
        S.act(sc["ab"][:], sc["ab"][:], AF.Ln, ["ab1"], ["ab"], bias=1.0)
        S.ts("dve", sc["sp"][:], sc["xa"][:], 0.0, None, ALU.max, None, ["xa"], ["sp0"])
        S.tt("dve", sc["sp"][:], sc["sp"][:], sc["ab"][:], ALU.add, ["sp0", "ab"], ["sp"])
        S.ts("dve", sc["glog"][:], sc["sp"][:], nexpA[:, 0:1], None, ALU.mult, None, ["sp", "nexpA"], ["glog"])
        gps = sm_ps[:, 16:48].rearrange("p (a b) -> p a b", a=4)
        for a, m in enumerate((LTbd, ONESbd, SEL0, SEL1)):
            S.mm(gps[:, a, :], m, sc["glog"][:], ["cmat", "glog"], ["gps"])
        S.copy("dve", gs[:], gps, ["gps"], ["gs"])
        S.act(sc["eg"][:], gs[:, 0, :], AF.Exp, ["gs"], ["eg"])
        S.tt("dve", sc["beg"][:], sc["beta"][:], sc["eg"][:], ALU.mult, ["beta", "eg"], ["beg"])
        S.tt("dve", sc["dd"][:], gs[:, 1, :], gs[:, 0, :], ALU.subtract, ["gs"], ["dd"])
        S.act(sc["ek"][:], sc["dd"][:], AF.Exp, ["dd"], ["ek"])
        S.act(ge[:], gs[:, 2:4, :], AF.Exp, ["gs"], ["ge"])


    def phase_a(s):
        for k in range(9):
            if k < 8:
                st0(s, k)
            if 1 <= k <= 8:
                st1(s, k - 1)
        for k in range(4):
            st2(s, k)
            tmstep(s, k)
        scalars(s)

    def par(s, n):
        sl = s % 2
        q = n % 2
        p_ = n % NPS
        w, r = W[q], R[p_]
        G_ps, KK_ps, KQ_ps, wT_ps = FA[q][:, 0:128], FA[q][:, 128:256], FA[q][:, 256:384], FA[q][:, 384:512]
        P_ps, Y_ps, Tn_ps = FB[q][:, 0:128], FB[q][:, 128:256], FB[q][:, 256:384]
        trb = H[q]
        kTn = fm["kT2"][sl][:, n, :, :].rearrange("p h i -> p (h i)")
        qTn = fm["qT2"][sl][:, n, :, :].rearrange("p h i -> p (h i)")
        vTn = fm["vT2"][sl][:, n, :, :].rearrange("p h i -> p (h i)")
        zTn = fm["zT2"][sl][:, n, :, :].rearrange("p h i -> p (h i)")
        kk = [("kT2", sl, 0), ("kT2", sl, 1)]
        qk_ = [("qT2", sl, 0), ("qT2", sl, 1)]
        vk = [("vT2", sl, 0), ("vT2", sl, 1)]
        zk = [("zT2", sl, 0), ("zT2", sl, 1)]
        K_ = lambda nm: (nm, q)
        Rk = lambda nm: (nm, p_)
        col = lambda t_: t_[:, n:n + 1]
        S.tr(trb[:, 0:128], kTn, identb[:], kk + ["identb"], [K_("trk")])
        S.tr(trb[:, 128:256], vTn, identb[:], vk + ["identb"], [K_("trv")])
        S.tr(trb[:, 256:384], zTn, identb[:], zk + ["identb"], [K_("trz")])
        S.ts("pool", w["dg"][:], identf, gs[:, 0, n:n + 1], None, ALU.mult, None, ["cmat", "gs"], [K_("dg")])
        S.mm(G_ps, onesf[:], w["dg"][:], ["onesf", K_("dg")], [K_("G")])
        S.mm(KK_ps, kTn, kTn, kk, [K_("KK")])
        S.mm(KQ_ps, kTn, qTn, kk + qk_, [K_("KQ")])
        yield
        S.ts("dve", w["kbeg"][:], trb[:, 0:128], col(sc["beg"]), None, ALU.mult, None, [K_("trk"), "beg"], [K_("kbeg")])
        S.act(r["kend0"][0:64, :], trb[0:64, 0:128], AF.Copy, [K_("trk"), "ek"], [Rk("kend0")], scale=sc["ek"][0:64, n:n + 1])
        S.act(r["kend1"][64:128, :], trb[64:128, 0:128], AF.Copy, [K_("trk"), "ek"], [Rk("kend1")], scale=sc["ek"][64:128, n:n + 1])
        yield
        S.act(r["vb"][:], trb[:, 128:256], AF.Copy, [K_("trv"), "beta"], [Rk("vb")], scale=col(sc["beta"]))
        S.tt("dve", r["gw"][:], trb[:, 256:384], onwb[:], ALU.mult, [K_("trz"), "onwb"], [Rk("gw")])
        yield
        S.stt("dve", w["a1"][:], G_ps, gs[:, 0, n:n + 1], maskS, ALU.subtract, ALU.add, [K_("G"), "gs", "cmat"], [K_("a1")])
        S.act(w["Erow"][:], G_ps, AF.Exp, [K_("G")], [K_("Erow")])
        yield
        S.stt("dve", w["a2"][:], G_ps, gs[:, 0, n:n + 1], maskTn, ALU.subtract, ALU.add, [K_("G"), "gs", "cmat"], [K_("a2")])
        S.act(w["Ds"][:], w["a1"][:], AF.Exp, [K_("a1")], [K_("Ds")], scale=-1.0)
        yield
        S.act(w["DTi"][:], w["a2"][:], AF.Exp, [K_("a2")], [K_("DTi")])
        S.stt("dve", w["X"][:], KK_ps, col(sc["nbeta"]), w["Ds"][:], ALU.mult, ALU.mult, [K_("KK"), "nbeta", K_("Ds")], [K_("X")])
        yield
        S.tr(trb[:, 384:512], w["X"][:], identb[:], [K_("X"), "identb"], [K_("trx")])
        S.tt("dve", r["attnT"][:], KQ_ps, w["DTi"][:], ALU.mult, [K_("KQ"), K_("DTi")], [Rk("attnT")])
        yield
        S.copy("act", w["XT"][:], trb[:, 384:512], [K_("trx")], [K_("XT")])
        S.tt("pool", r["qd0"][:, 0:64], qTn[:, 0:64], w["Erow"][:, 0:64], ALU.mult, qk_ + [K_("Erow")], [Rk("qd0")])
        S.tt("pool", r["qd1"][:, 64:128], qTn[:, 64:128], w["Erow"][:, 64:128], ALU.mult, qk_ + [K_("Erow")], [Rk("qd1")])
        yield
        S.tt("pool", w["TTb"][0][:], identb[:], w["XT"][:], ALU.add, ["identb", K_("XT")], [K_("TT0")])
        Pc, Pk, Yc, Yk = w["X"], K_("X"), w["XT"], K_("XT")
        ti = 0
        pend = None
        for k in range(1, 6):
            pi = k % 2
            S.mm(P_ps, Yc[:], Pc[:], [Yk, Pk], [K_("Pps")])
            if k < 5:
                S.mm(Y_ps, Pc[:], Yc[:], [Yk, Pk], [K_("Yps")])
            if pend is not None:
                S.mm(Tn_ps, pend[0][:], w["TTb"][ti][:], [pend[1], K_("TT%d" % ti)], [K_("Tps")])
            yield
            S.copy("act", w["Pb"][pi][:], P_ps, [K_("Pps")], [K_("Pb%d" % pi)])
            if k < 5:
                S.copy("dve", w["Yb"][pi][:], Y_ps, [K_("Yps")], [K_("Yb%d" % pi)])
            if pend is not None:
                S.tt("dve", w["TTb"][1 - ti][:], w["TTb"][ti][:], Tn_ps, ALU.add, [K_("TT%d" % ti), K_("Tps")], [K_("TT%d" % (1 - ti))])
                ti = 1 - ti
            pend = (w["Pb"][pi], K_("Pb%d" % pi))
            Pc, Pk, Yc, Yk = w["Pb"][pi], K_("Pb%d" % pi), w["Yb"][pi], K_("Yb%d" % pi)
            yield
        S.mm(Tn_ps, pend[0][:], w["TTb"][ti][:], [pend[1], K_("TT%d" % ti)], [K_("Tps")])
        yield
        S.tt("dve", r["TT"][:], w["TTb"][ti][:], Tn_ps, ALU.add, [K_("TT%d" % ti), K_("Tps")], [Rk("TT")])
        yield
        S.mm(wT_ps, w["kbeg"][:], r["TT"][:], [K_("kbeg"), Rk("TT")], [K_("wT")])
        yield
        S.ts("dve", r["nw0"][:, 0:64], wT_ps[:, 0:64], -1.0, None, ALU.mult, None, [K_("wT")], [Rk("nw0")])
        S.act(r["nw1"][:, 64:128], wT_ps[:, 64:128], AF.Copy, [K_("wT")], [Rk("nw1")], scale=-1.0)
        yield

    def scan(s, n):
        sl = s % 2
        p_ = n % NPS
        r = R[p_]
        Rk = lambda nm: (nm, p_)
        vn_ps, o_ps, kv_ps = FS[:, 0:128], FS[:, 128:256], FS[:, 256:512]
        S.mm(vn_ps, r["TT"][:], r["vb"][:], [Rk("TT"), Rk("vb")], ["vn"], start=True, stop=False)
        S.mm(vn_ps, r["nw0"][:], Sbf[:, 0:128], [Rk("nw0"), "Sbf"], ["vn"], start=False, stop=False)
        S.mm(vn_ps, r["nw1"][:], Sbf[:, 128:256], [Rk("nw1"), "Sbf"], ["vn"], start=False, stop=True)
        yield
        S.copy("act", vnb[:], vn_ps, ["vn"], ["vnb"])
        yield
        S.mm(o_ps, r["qd0"][:], Sbf[:, 0:128], [Rk("qd0"), "Sbf"], ["o"], start=True, stop=False)
        S.mm(o_ps, r["qd1"][:], Sbf[:, 128:256], [Rk("qd1"), "Sbf"], ["o"], start=False, stop=False)
        S.mm(o_ps, r["attnT"][:], vnb[:], [Rk("attnT"), "vnb"], ["o"], start=False, stop=True)
        S.mm(kv_ps[:, 0:128], r["kend0"][:], vnb[:], [Rk("kend0"), "vnb"], ["kv"])
        S.mm(kv_ps[:, 128:256], r["kend1"][:], vnb[:], [Rk("kend1"), "vnb"], ["kv"])
        yield
        for hh in range(2):
            S.stt("dve", Sf[:, hh * 128:(hh + 1) * 128], Sf[:, hh * 128:(hh + 1) * 128], ge[:, hh, n:n + 1],
                  kv_ps[:, hh * 128:(hh + 1) * 128], ALU.mult, ALU.add, ["Sf", "ge", "kv"], ["Sf"])
        S.act(osq[:], o_ps, AF.Square, ["o"], ["osq"])
        yield
        S.copy("pool", Sbf[:], Sf[:], ["Sf"], ["Sbf"])
        S.P.op("dve", lambda e: e.reduce_sum(out=oss[:], in_=osq[:], axis=AX.X), ["osq"], ["oss"])
        yield
        S.rsqrt(orstd[:], oss[:], ["oss"], ["orstd"], 1.0 / 128, EPS, ol[:])
        yield
        S.stt("dve", ysb[:], o_ps, orstd[:, 0:1], r["gw"][:], ALU.mult, ALU.mult, ["o", "orstd", Rk("gw")], ["ysb"])
        yield
        S.tr(HS[:, 0:128], ysb[:], identb[:], ["ysb", "identb"], ["try"])
        yield
        S.copy("act", ybuf[sl][:, :, n, :], HS[:, 0:128].rearrange("p (h i) -> p h i", h=2), ["try"], [("ybuf", sl, n)])
        yield

    def drive(gens):
        gens = list(gens)
        while gens:
            for g_ in list(gens):
                try:
                    next(g_)
                except StopIteration:
                    gens.remove(g_)

    def chain(*gs_):
        for g_ in gs_:
            yield from g_

    def phase_b_all(s):
        for kp in range(5):
            gens = []
            if kp < 4:
                gens += [par(s, 2 * kp), par(s, 2 * kp + 1)]
            if kp >= 1:
                gens.append(chain(scan(s, 2 * kp - 2), scan(s, 2 * kp - 1)))
            drive(gens)

    try:
        ck(0)
        ldx(0)
        ck(1)
        front(0)

        def issue_ag(k):
            keys = [("out", "st_y%d%d" % (a_, b_)) for a_ in range(2) for b_ in range(2)]
            S.P.op("pool", lambda e: e.collective_compute("AllGather", ALU.bypass, replica_groups=RG,
                                                          ins=[y0T.t[k].opt()], outs=[ag_dst.t[k].opt()]),
                   keys, [("agout", k)], dma="cc%d" % k, inc=1)

        for s in range(nsup):
            if s + 1 < nsup:
                ldx(s + 1)
            phase_a(s)
            if ag_dst is not None and s >= 4 and s % 4 == 0:
                issue_ag(s // 4 - 1)
            if s + 1 < nsup:
                front(s + 1)
            phase_b_all(s)
            for hh in range(2):
                S.store(y0T.tile(s)[0][hh * 128:(hh + 1) * 128, :],
                        ybuf[s % 2][:, hh, :, :].rearrange("p n i -> p (n i)"),
                        [("ybuf", s % 2, n) for n in range(8)], "st_y%d%d" % (s % 2, hh))
        if ag_dst is not None:
            issue_ag(NCHK - 1)
            S.P.op("pool", None, [("agout", k) for k in range(NCHK)], ())
    except _Stop:
        pass
    return S.finish()


def gdn_inputs(xT_b, a_norm_w, a_w_in, a_conv_w, a_A_log, a_dt_bias, a_o_norm_w, g):
    h0, h1 = 2 * g, 2 * g + 1
    cols, ccols = [], []
    for base in (0, 1024, 2048, 3072):
        for h in (h0, h1):
            cols.extend(range(base + h * 128, base + (h + 1) * 128))
            if base < 3072:
                ccols.append((base + h * 128, base + (h + 1) * 128))
    cols.extend([4096 + h0, 4104 + h0, 4096 + h1, 4104 + h1])
    cw = np.stack([a_conv_w[0][:, a:b].T for (a, b) in ccols], axis=1)
    hconst = np.zeros((128, 2), np.float32)
    hconst[0:64, 0], hconst[64:128, 0] = a_A_log[0][h0], a_A_log[0][h1]
    hconst[0:64, 1], hconst[64:128, 1] = a_dt_bias[0][h0], a_dt_bias[0][h1]
    p = np.arange(128)
    hh, ii = p // 64, p % 64
    same = hh[:, None] == hh[None, :]
    ident = np.eye(128, dtype=np.float32)
    maskS = np.where(same & (ii[:, None] > ii[None, :]), 0.0, BIG).astype(np.float32)
    maskTn = np.where(same & (ii[None, :] >= ii[:, None]), 0.0, -BIG).astype(np.float32)
    LTbd = (same & (ii[:, None] <= ii[None, :])).astype(np.float32)
    ONESbd = same.astype(np.float32)
    SEL0 = np.repeat((hh == 0)[:, None], 128, 1).astype(np.float32)
    SEL1 = np.repeat((hh == 1)[:, None], 128, 1).astype(np.float32)
    swm = (p[:, None] == ((p[None, :] + 64) % 128)).astype(np.float32)
    return {
        "xT": xT_b,
        "win": np.ascontiguousarray(a_w_in[0][:, cols]),
        "normw": np.ascontiguousarray(a_norm_w[0].reshape(8, 128).T),
        "convw": np.ascontiguousarray(cw.reshape(128, 24)),
        "hconst": hconst,
        "onw": np.ascontiguousarray(a_o_norm_w[0].reshape(1, 128)),
        "cmats": np.stack([ident, maskS, maskTn, LTbd, ONESbd, SEL0, SEL1]),
        "swm": swm,
    }


def build_fused(upto="f", gdn_nsup=T // 512):
    nc = bass.Bass("TRN2", target_bir_lowering=False)

    def din(name, shape, dt=F32):
        return nc.dram_tensor(name, list(shape), dt, kind="ExternalInput").ap()

    def dint(name, shape, dt):
        return nc.dram_tensor(name, list(shape), dt).ap()

    xT = din("xT", [D, T]); xTq = din("xTq", [256, T])
    a_win = din("a_win", [D, 1028]); a_normw = din("a_normw", [128, 8]); convw = din("convw", [128, 24])
    hconst = din("hconst", [128, 2]); onw = din("onw", [1, 128]); cmats = din("cmats", [7, 128, 128])
    swm = din("swm", [128, 128])
    a_wo = din("a_wo", [D, 256]); b_normw2 = din("b_normw2", [128, 2])
    b_win = din("b_win", [D, 1026]); fbias = din("fbias", [1, 2]); qkw = din("qkw", [1, 512]); ut = din("ut", [128, 128])
    b_wo = din("b_wo", [D, 256]); f_normw2 = din("f_normw2", [128, 2])
    outT = nc.dram_tensor("outT", [256, T], F32, kind="ExternalOutput").ap()

    y0_src = Chunked(nc, "y0_src", 256, BF16); y0_all = Chunked(nc, "y0_all", D, BF16)
    h1 = dint("h1", [256, T], F32); ss1_src = dint("ss1_src", [1, T], F32); ss1_all = dint("ss1_all", [4, T], F32)
    xn_src = Chunked(nc, "xn_src", 256, BF16); xn_all = Chunked(nc, "xn_all", D, BF16)
    y1_src = Chunked(nc, "y1_src", 256, BF16); y1_all = Chunked(nc, "y1_all", D, BF16)
    h2 = dint("h2", [256, T], F32); ss2_src = dint("ss2_src", [1, T], F32); ss2_all = dint("ss2_all", [4, T], F32)

    S = Stage(nc, "a_")
    build_gdn_stage(S, xT, a_win, a_normw, convw, hconst, onw, cmats, swm, y0_src, nsup=gdn_nsup, ag_dst=y0_all)
    cc_a = [S.P.dsem["cc%d" % k] for k in range(NCHK)] if gdn_nsup == T // 512 else None
    if upto == "a":
        return nc

    S = Stage(nc, "b_")
    for k in range(NCHK):
        S.P.ext(("ag", k), cc_a[k])
    build_rows_proj(S, y0_all, xTq, a_wo, h1, ss1_src)
    S.finish()
    if upto == "b":
        return nc

    S = Stage(nc, "c_")
    allgather(S, ss1_src, ss1_all, "ag", "cc")
    build_rows_norm(S, h1, ss1_all, b_normw2, xn_src, BF16, in_key=["ag"])
    S.finish()
    if upto == "c":
        return nc

    S = Stage(nc, "d_")
    allgather_chunked(S, xn_src, xn_all)
    build_fox_stage(S, xn_all, b_win, fbias, qkw, cmats[0], ut, y1_src, ag_dst=y1_all)
    cc_d = [S.P.dsem["cc%d" % k] for k in range(NCHK)]
    if upto == "d":
        return nc

    S = Stage(nc, "e_")
    for k in range(NCHK):
        S.P.ext(("ag", k), cc_d[k])
    build_rows_proj(S, y1_all, h1, b_wo, h2, ss2_src)
    S.finish()

    S = Stage(nc, "f_")
    allgather(S, ss2_src, ss2_all, "ag", "cc")
    build_rows_norm(S, h2, ss2_all, f_normw2, outT, F32, in_key=["ag"])
    S.finish()
    return nc


def kernel(x, a_norm_w, a_w_in, a_conv_w, a_A_log, a_dt_bias, a_o_norm_w, a_w_out,
           b_norm_w, b_w_in, b_f_bias, b_q_norm_w, b_k_norm_w, b_w_out, final_norm_w):
    f32 = lambda a: np.ascontiguousarray(np.asarray(a, dtype=np.float32))
    x = f32(x)
    (a_norm_w, a_w_in, a_conv_w, a_A_log, a_dt_bias, a_o_norm_w, a_w_out, b_norm_w, b_w_in, b_f_bias,
     b_q_norm_w, b_k_norm_w, b_w_out, final_norm_w) = [f32(a) for a in (
        a_norm_w, a_w_in, a_conv_w, a_A_log, a_dt_bias, a_o_norm_w, a_w_out, b_norm_w, b_w_in, b_f_bias,
        b_q_norm_w, b_k_norm_w, b_w_out, final_norm_w)]
    B = x.shape[0]
    xT = [np.ascontiguousarray(x[b].T) for b in range(B)]
    nc = _get("fused", build_fused)
    maps = []
    for c in range(NCORE):
        b, g = c // 4, c % 4
        fs = slice(g * 256, (g + 1) * 256)
        m = gdn_inputs(xT[b], a_norm_w, a_w_in, a_conv_w, a_A_log, a_dt_bias, a_o_norm_w, g)
        m["a_win"] = m.pop("win")
        m["a_normw"] = m.pop("normw")
        fi = fox_inputs(None, b_w_in, b_f_bias, b_q_norm_w, b_k_norm_w, g)
        m.update({
            "xTq": np.ascontiguousarray(xT[b][fs]),
            "a_wo": np.ascontiguousarray(a_w_out[0][:, fs]),
            "b_normw2": np.ascontiguousarray(b_norm_w[0][fs].reshape(2, 128).T),
            "b_win": fi["win"], "fbias": fi["fbias"], "qkw": fi["qkw"], "ut": fi["ut"],
            "b_wo": np.ascontiguousarray(b_w_out[0][:, fs]),
            "f_normw2": np.ascontiguousarray(final_norm_w[fs].reshape(2, 128).T),
        })
        maps.append(m)
    res = _run(nc, maps)
    out = np.empty((B, T, D), np.float32)
    for c in range(NCORE):
        b, g = c // 4, c % 4
        out[b, :, g * 256:(g + 1) * 256] = res[c]["outT"].T
    return out
```

```python
from contextlib import ExitStack

import numpy as np
import ml_dtypes
import concourse.bass as bass
import concourse.mybir as mybir
from concourse.bass_utils import run_bass_kernel_spmd

F32 = mybir.dt.float32
BF16 = mybir.dt.bfloat16
AF = mybir.ActivationFunctionType
ALU = mybir.AluOpType
AX = mybir.AxisListType

T = 8192
D = 1024
NCORE = 8
EPS = 1e-6
NO_POOL_ALU = True


class Prog:
    ENGS = ("pe", "act", "dve", "pool", "sp")

    def __init__(self, nc):
        self.nc = nc
        self.ops = []

    def op(self, eng, fn, r=(), w=(), dma=None, inc=16):
        self.ops.append(dict(eng=eng, fn=fn, r=tuple(r), w=tuple(w), dma=dma, inc=inc))

    def ext(self, key, sem, value=1):
        self.ops.append(dict(eng="pool", fn=None, r=(), w=(key,), dma=("ext", len(self.ops)), inc=value, ext_sem=sem))

    def emit(self, prefix=""):
        nc = self.nc
        ops = self.ops
        last_w, readers = {}, {}
        for idx, o in enumerate(ops):
            deps = set()
            for k in o["r"]:
                if k in last_w:
                    deps.add(last_w[k])
            for k in o["w"]:
                if k in last_w:
                    deps.add(last_w[k])
                deps.update(readers.get(k, ()))
            for k in o["r"]:
                readers.setdefault(k, []).append(idx)
            for k in o["w"]:
                last_w[k] = idx
                readers[k] = []
            deps.discard(idx)
            eng_dep, dma_dep = {}, {}
            for d in deps:
                od = ops[d]
                if od["dma"] is not None:
                    g = od["dma"]
                    dma_dep[g] = max(dma_dep.get(g, -1), d)
                else:
                    if od["eng"] == "pe" and o["eng"] == "pe" and o["dma"] is None:
                        continue
                    e = od["eng"]
                    eng_dep[e] = max(eng_dep.get(e, -1), d)
            o["edeps"] = eng_dep
            o["ddeps"] = dma_dep
            for d in eng_dep.values():
                ops[d]["sig"] = True
        ecount = {e: 0 for e in self.ENGS}
        dcount = {}
        ext = {}
        for o in ops:
            if o.get("ext_sem") is not None:
                o["cnt"] = 1
                ext[o["dma"]] = o["ext_sem"]
            elif o["dma"] is not None:
                g = o["dma"]
                dcount[g] = dcount.get(g, 0) + 1
                o["cnt"] = dcount[g]
            elif o.get("sig"):
                ecount[o["eng"]] += 1
                o["cnt"] = ecount[o["eng"]]
        esem = {e: nc.alloc_semaphore(prefix + "sem_" + e) for e in self.ENGS}
        dsem = {g: nc.alloc_semaphore(prefix + "dsem_" + str(g)) for g in dcount}
        engobj = {"pe": "tensor", "act": "scalar", "dve": "vector", "pool": "gpsimd", "sp": "sync"}
        self.sems = list(esem.values()) + [h_ for g, h_ in dsem.items() if not str(g).startswith("cc")]
        self.dsem = dict(dsem)
        dsem.update(ext)
        with nc.Block() as block:
            for ename in self.ENGS:
                def body(e, ename=ename):
                    known = {}
                    for o in ops:
                        if o["eng"] != ename:
                            continue
                        for de, d in o["edeps"].items():
                            v = ops[d]["cnt"]
                            if known.get(("e", de), 0) < v:
                                e.wait_ge(esem[de], v)
                                known[("e", de)] = v
                        for g, d in o["ddeps"].items():
                            v = ops[d]["inc"] * ops[d]["cnt"]
                            if known.get(("d", g), 0) < v:
                                e.wait_ge(dsem[g], v)
                                known[("d", g)] = v
                        if o["fn"] is None:
                            continue
                        ins = o["fn"](e)
                        if o["dma"] is not None:
                            ins.then_inc(dsem[o["dma"]], o["inc"])
                        elif o.get("sig"):
                            ins.then_inc(esem[ename], 1)
                getattr(block, engobj[ename])(body)


class PS:
    def __init__(self, ap, bank):
        self.ap, self.bank = ap, bank

    def __getitem__(self, idx):
        return PS(self.ap[idx], self.bank)

    def rearrange(self, *a, **k):
        return PS(self.ap.rearrange(*a, **k), self.bank)


def _u(w, *xs):
    w = list(w)
    outs = []
    for x in xs:
        if isinstance(x, PS):
            k = ("bank", x.bank)
            if k not in w:
                w.append(k)
            outs.append(x.ap)
        else:
            outs.append(x)
    return outs, w


class Stage:
    def __init__(self, nc=None, prefix=""):
        self.nc = nc if nc is not None else bass.Bass("TRN2", target_bir_lowering=False)
        self.P = Prog(self.nc)
        self._n = 0
        self.out_keys = []
        self.prefix = prefix
        self.stack = ExitStack()

    def din(self, name, shape, dt=F32):
        return self.nc.dram_tensor(name, list(shape), dt, kind="ExternalInput").ap()

    def dout(self, name, shape, dt=F32):
        return self.nc.dram_tensor(name, list(shape), dt, kind="ExternalOutput").ap()

    def sb(self, shape, dt=F32, name="t"):
        self._n += 1
        return self.stack.enter_context(self.nc.sbuf_tensor("%s%s_%d" % (self.prefix, name, self._n), list(shape), dt))

    def bank(self, dt=F32):
        self._n += 1
        n = 512 if dt == F32 else 1024
        t = self.stack.enter_context(self.nc.psum_tensor("%sps_%d" % (self.prefix, self._n), [128, n], dt))
        return PS(t, self._n)

    def mm(self, out, lhsT, rhs, r, w, start=True, stop=True):
        (out,), w = _u(w, out)
        self.P.op("pe", lambda e: e.matmul(out, lhsT=lhsT, rhs=rhs, start=start, stop=stop,
                                           skip_group_check=True), r, w)

    def tr(self, out, in_, ident, r, w):
        (out,), w = _u(w, out)
        self.P.op("pe", lambda e: e.transpose(out, in_, ident), r, w)

    def act(self, out, in_, func, r, w, bias=None, scale=None, accum=None):
        kw = {}
        if bias is not None:
            kw["bias"] = bias
        if scale is not None:
            kw["scale"] = scale
        if accum is not None:
            kw["accum_out"] = accum
        (out, in_), w = _u(w, out, in_)
        self.P.op("act", lambda e: e.activation(out=out, in_=in_, func=func, **kw), r, w)

    def copy(self, eng, out, in_, r, w):
        if eng == "act":
            self.act(out, in_, AF.Copy, r, w)
        else:
            (out, in_), w = _u(w, out, in_)
            self.P.op(eng, lambda e: e.tensor_copy(out=out, in_=in_), r, w)

    def tt(self, eng, out, in0, in1, op, r, w):
        (out, in0, in1), w = _u(w, out, in0, in1)
        self.P.op(eng, lambda e: e.tensor_tensor(out=out, in0=in0, in1=in1, op=op), r, w)

    def ts(self, eng, out, in0, s1, s2, op0, op1, r, w):
        if eng == "pool" and NO_POOL_ALU:
            eng = "dve"
        (out, in0), w = _u(w, out, in0)
        if op1 is None:
            self.P.op(eng, lambda e: e.tensor_scalar(out=out, in0=in0, scalar1=s1, scalar2=None, op0=op0), r, w)
        else:
            self.P.op(eng, lambda e: e.tensor_scalar(out=out, in0=in0, scalar1=s1, scalar2=s2, op0=op0, op1=op1), r, w)

    def stt(self, eng, out, in0, scalar, in1, op0, op1, r, w):
        if eng == "pool" and NO_POOL_ALU:
            eng = "dve"
        (out, in0, in1), w = _u(w, out, in0, in1)
        self.P.op(eng, lambda e: e.scalar_tensor_tensor(out=out, in0=in0, scalar=scalar, in1=in1, op0=op0, op1=op1), r, w)

    def recip(self, out, in_, r, w):
        (out, in_), w = _u(w, out, in_)
        self.P.op("dve", lambda e: e.reciprocal(out=out, in_=in_), r, w)

    def memset(self, eng, ap, val, w):
        self.P.op(eng, lambda e: e.memset(ap, val), (), w)

    def load(self, out, in_, w, group, eng="sp", r=()):
        self.P.op(eng, lambda e: e.dma_start(out=out, in_=in_), r, w, dma=group)

    def store(self, out, in_, r, group, eng="sp"):
        key = ("out", group)
        self.P.op(eng, lambda e: e.dma_start(out=out, in_=in_), r, (key,), dma=group)
        if key not in self.out_keys:
            self.out_keys.append(key)

    def finish(self):
        self.P.op("sp", None, self.out_keys, ())
        self.P.emit(self.prefix)
        self.stack.close()
        return self.nc

    def rsqrt(self, out, in_, r, w, scale, bias, tmp):
        self.act(tmp, in_, AF.Ln, r, [("tmp", id(tmp))], bias=bias, scale=scale)
        self.act(out, tmp, AF.Exp, [("tmp", id(tmp))], w, scale=-0.5)

    def load_weight_bf16(self, w_dram, ncols, name, scale_col=None, scale_mul=None):
        wb = self.sb([128, 8, ncols], BF16, name)
        stg = [self.sb([128, ncols], F32, name + "_stg") for _ in range(2)]
        wv = w_dram.rearrange("(fc p) c -> p fc c", p=128)
        for fc in range(8):
            s = stg[fc % 2]
            sk = (name + "_stg", fc % 2)
            self.load(s[:], wv[:, fc, :], [sk], name + "_ld%d" % (fc % 2))
            eng = "dve" if fc % 2 == 0 else "pool"
            if scale_col is None:
                self.copy(eng, wb[:, fc, :], s[:], [sk], [(name, fc)])
            else:
                self.ts(eng, wb[:, fc, :], s[:], scale_col[:, fc:fc + 1], scale_mul, ALU.mult, ALU.mult,
                        [sk, "normw"], [(name, fc)])
        return wb, [(name, fc) for fc in range(8)]


RG = [[0, 1, 2, 3], [4, 5, 6, 7]]
NCHK = 4


class Chunked:
    def __init__(self, nc, name, rows, dt):
        self.w = T // NCHK
        self.t = [nc.dram_tensor("%s%d" % (name, k), [rows, self.w], dt).ap() for k in range(NCHK)]

    def tile(self, s):
        k = (s * 512) // self.w
        off = s * 512 - k * self.w
        return self.t[k][:, off:off + 512], k


def allgather_chunked(S, src, dst):
    for k in range(NCHK):
        allgather(S, src.t[k], dst.t[k], ("ag", k), "cc%d" % k)


def allgather(S, src, dst, key, group):
    S.P.op("pool", lambda e: e.collective_compute("AllGather", ALU.bypass, replica_groups=RG,
                                                  ins=[src.opt()], outs=[dst.opt()]),
           (), [key], dma=group, inc=1)


def build_rows_proj(S, yall, resrows, wcols, hrows, ss_src, in_key=()):
    wb, wkeys = S.load_weight_bf16(wcols, 256, "wo")
    ones = S.sb([128, 128], BF16, "ones")
    S.memset("pool", ones[:], 1.0, ["ones"])
    rv = resrows.rearrange("(oc p) t -> p oc t", p=128)
    hv = hrows.rearrange("(oc p) t -> p oc t", p=128)
    yt = [S.sb([128, 8, 512], BF16, "yt") for _ in range(2)]
    rt = [S.sb([128, 2, 512], F32, "rt") for _ in range(2)]
    h = [S.sb([128, 2, 512], F32, "h") for _ in range(2)]
    sq = S.sb([128, 2, 512], BF16, "sq")
    ssrow = S.sb([1, T], F32, "ssrow")
    pb = [S.bank(), S.bank()]
    ssb = S.bank()
    NSUP = T // 512

    def ld(s):
        sl = s % 2
        yap, k = yall.tile(s)
        S.load(yt[sl][:], yap.rearrange("(fc p) t -> p fc t", p=128), [("yt", sl)], "ld_y%d" % sl, r=[("ag", k)])
        S.load(rt[sl][:], rv[:, :, s * 512:(s + 1) * 512], [("rt", sl)], "ld_r%d" % sl)

    ld(0)
    for s in range(NSUP):
        sl = s % 2
        if s + 1 < NSUP:
            ld(s + 1)
        for oc in range(2):
            for fc in range(8):
                S.mm(pb[oc][:], wb[:, fc, oc * 128:(oc + 1) * 128], yt[sl][:, fc, :], [wkeys[fc], ("yt", sl)],
                     [("pb", oc)], start=(fc == 0), stop=(fc == 7))
            S.tt("dve", h[sl][:, oc, :], pb[oc][:], rt[sl][:, oc, :], ALU.add, [("pb", oc), ("rt", sl)], [("h", sl, oc)])
            S.act(sq[:, oc, :], h[sl][:, oc, :], AF.Square, [("h", sl, oc)], [("sq", oc)])
        for oc in range(2):
            S.mm(ssb[:], ones[:], sq[:, oc, :], ["ones", ("sq", oc)], ["ssb"], start=(oc == 0), stop=(oc == 1))
        S.copy("act", ssrow[0:1, s * 512:(s + 1) * 512], ssb[0:1, :], ["ssb"], [("ssrow", s)])
        S.store(hv[:, :, s * 512:(s + 1) * 512], h[sl][:], [("h", sl, 0), ("h", sl, 1)], "st_h%d" % sl)
    S.store(ss_src, ssrow[:], [("ssrow", s) for s in range(NSUP)], "st_ss")


def build_rows_norm(S, hrows, ss_all, normw2, dst, dst_dt, in_key=()):
    nw = S.sb([128, 2], F32, "nw")
    S.load(nw[:], normw2, ["nw"], "ld_nw")
    onesf = S.sb([128, 128], F32, "onesf")
    S.memset("pool", onesf[:], 1.0, ["onesf"])
    hv = hrows.rearrange("(oc p) t -> p oc t", p=128)
    ssz = [S.sb([128, 512], F32, "ssz") for _ in range(2)]
    for z in range(2):
        S.memset("pool", ssz[z][:], 0.0, [("ssz", z)])
    h = [S.sb([128, 2, 512], F32, "h") for _ in range(2)]
    xo = [S.sb([128, 2, 512], dst_dt, "xo") for _ in range(2)]
    lt = S.sb([128, 512], F32, "lt")
    rstd = S.sb([128, 512], F32, "rstd")
    tb = S.bank()
    NSUP = T // 512

    def ld(s):
        sl = s % 2
        S.load(ssz[sl][0:4, :], ss_all[:, s * 512:(s + 1) * 512], [("ssz", sl)], "ld_s%d" % sl, r=in_key)
        S.load(h[sl][:], hv[:, :, s * 512:(s + 1) * 512], [("h", sl)], "ld_h%d" % sl)

    ld(0)
    for s in range(NSUP):
        sl = s % 2
        if s + 1 < NSUP:
            ld(s + 1)
        S.mm(tb[:], onesf[:], ssz[sl][:], ["onesf", ("ssz", sl)], ["tb"])
        S.rsqrt(rstd[:], tb[:], ["tb"], ["rstd"], 1.0 / D, EPS, lt[:])
        for oc in range(2):
            S.stt("dve", xo[sl][:, oc, :], h[sl][:, oc, :], nw[:, oc:oc + 1], rstd[:], ALU.mult, ALU.mult,
                  [("h", sl), "rstd", "nw"], [("xo", sl, oc)])
        dap = dst.tile(s)[0] if isinstance(dst, Chunked) else dst[:, s * 512:(s + 1) * 512]
        S.store(dap.rearrange("(oc p) t -> p oc t", p=128), xo[sl][:], [("xo", sl, 0), ("xo", sl, 1)], "st_x%d" % sl)


def build_fox_stage(S, xnT, win, fbias, qkw, ident_d, ut_d, y1T, in_key=(), ag_dst=None):

    NT = T // 128
    NQB = T // 512
    scale = 128.0 ** -0.5

    wb, wkeys = S.load_weight_bf16(win, 1026, "wi")
    identf = S.sb([128, 128], F32, "identf")
    identb = S.sb([128, 128], BF16, "identb")
    ut = S.sb([128, 128], F32, "ut")
    onesf = S.sb([128, 128], F32, "onesf")
    maskT = S.sb([128, 128], BF16, "maskT")
    fb = S.sb([128, 2], F32, "fb")
    nwq = S.sb([128, 512], F32, "nwq")
    S.load(identf[:], ident_d, ["identf"], "ld_c0")
    S.load(ut[:], ut_d, ["ut"], "ld_c1")
    S.load(fb[:], fbias.partition_broadcast(128), ["fb"], "ld_c2")
    S.load(nwq[:], qkw.partition_broadcast(128), ["nwq"], "ld_c3")
    S.copy("dve", identb[:], identf[:], ["identf"], ["identb"])
    S.copy("dve", maskT[:], ut[:], ["ut"], ["maskT"])
    S.memset("pool", onesf[:], 1.0, ["onesf"])

    QKT = S.sb([128, 4, T], BF16, "QKT")
    V = S.sb([128, NT, 2, 129], BF16, "V")
    G = S.sb([128, NT, 256], BF16, "G")
    Fraw = S.sb([128, NT, 2], F32, "Fraw")
    S.memset("pool", V[:, :, :, 128:129], 1.0, ["Vones"])

    xb = [S.sb([128, 8, 512], BF16, "xb") for _ in range(2)]
    sqk = S.sb([128, 512], F32, "sqk")
    ss4 = S.sb([128, 4], F32, "ss4")
    l4 = S.sb([128, 4], F32, "l4")
    r4 = S.sb([128, 4], F32, "r4")
    qkn = [S.sb([128, 4, 128], BF16, "qkn") for _ in range(2)]

    banks = [S.bank() for _ in range(6)]
    tpbs = [S.bank(BF16), S.bank(BF16)]
    tpb = tpbs[0]

    def ldx(s):
        xap, k = xnT.tile(s)
        S.load(xb[s % 2][:], xap.rearrange("(fc p) t -> p fc t", p=128), [("xb", s % 2)], "ld_x%d" % (s % 2),
               r=[("ag", k)])

    def emit_tr(t):
        sl = t % 2
        for seg in range(4):
            S.tr(tpbs[sl][:, seg * 128:(seg + 1) * 128], qkn[sl][:, seg, :], identb[:],
                 [("qkn", sl, seg), "identb"], [("tp", sl)])
        S.copy("dve", QKT[:, :, t * 128:(t + 1) * 128],
               tpbs[sl][:, 0:512].rearrange("p (a b) -> p a b", a=4), [("tp", sl)], [("QKT", t)])

    pending = []
    ldx(0)
    for s in range(T // 512):
        if s + 1 < T // 512:
            ldx(s + 1)
        for j in range(4):
            t = 4 * s + j
            sl = t % 2
            qk_ps, vz_ps, f_ps = banks[sl], banks[2 + sl], banks[4 + sl]
            xk = ("xb", s % 2)
            for fc in range(8):
                lhs = xb[s % 2][:, fc, j * 128:(j + 1) * 128]
                S.mm(qk_ps[:], lhs, wb[:, fc, 0:512], [xk, wkeys[fc]], [("qk", sl)], start=(fc == 0), stop=(fc == 7))
            for fc in range(8):
                lhs = xb[s % 2][:, fc, j * 128:(j + 1) * 128]
                S.mm(vz_ps[:], lhs, wb[:, fc, 512:1024], [xk, wkeys[fc]], [("vz", sl)], start=(fc == 0), stop=(fc == 7))
            for fc in range(8):
                lhs = xb[s % 2][:, fc, j * 128:(j + 1) * 128]
                S.mm(f_ps[:, 0:2], lhs, wb[:, fc, 1024:1026], [xk, wkeys[fc]], [("fp", sl)], start=(fc == 0), stop=(fc == 7))
            S.act(sqk[:], qk_ps[:], AF.Square, [("qk", sl)], ["sqk"])
            S.P.op("dve", lambda e: e.reduce_sum(out=ss4[:], in_=sqk[:].rearrange("p (a b) -> p a b", a=4), axis=AX.X),
                   ["sqk"], ["ss4"])
            S.rsqrt(r4[:], ss4[:], ["ss4"], ["r4"], 1.0 / 128, EPS, l4[:])
            for seg in range(4):
                S.stt("dve", qkn[sl][:, seg, :], qk_ps[:, seg * 128:(seg + 1) * 128], r4[:, seg:seg + 1],
                      nwq[:, seg * 128:(seg + 1) * 128], ALU.mult, ALU.mult,
                      [("qk", sl), "r4", "nwq"], [("qkn", sl, seg)])
            S.copy("act", V[:, t, :, 0:128], vz_ps[:, 0:256].rearrange("p (a b) -> p a b", a=2), [("vz", sl)], [("V", t)])
            S.copy("act", G[:, t, :], vz_ps[:, 256:512], [("vz", sl)], [("G", t)])
            S.copy("dve", Fraw[:, t, :], f_ps[:, 0:2], [("fp", sl)], [("Fraw", t)])
            pending.append(t)
            if len(pending) > 1:
                emit_tr(pending.pop(0))
    while pending:
        emit_tr(pending.pop(0))

    for t8 in range(NT // 8):
        ks = [("G", t) for t in range(t8 * 8, t8 * 8 + 8)]
        S.act(G[:, t8 * 8:(t8 + 1) * 8, :], G[:, t8 * 8:(t8 + 1) * 8, :], AF.Silu, ks, ks)
    fk = [("Fraw", t) for t in range(NT)]
    xf = S.sb([128, NT, 2], F32, "xf")
    ab = S.sb([128, NT, 2], F32, "ab")
    L = S.sb([128, NT, 2], F32, "L")
    S.tt("dve", xf[:], Fraw[:], fb[:].unsqueeze(1).to_broadcast([128, NT, 2]), ALU.add, fk + ["fb"], ["xf"])
    S.act(ab[:], xf[:], AF.Abs, ["xf"], ["ab"])
    S.act(ab[:], ab[:], AF.Exp, ["ab"], ["ab"], scale=-1.0)
    S.act(ab[:], ab[:], AF.Ln, ["ab"], ["ab"], bias=1.0)
    S.ts("dve", xf[:], xf[:], 0.0, None, ALU.min, None, ["xf"], ["xf"])
    S.tt("dve", L[:], xf[:], ab[:], ALU.subtract, ["xf", "ab"], ["L"])
    cs_ps, tot_ps = banks[0], banks[1]
    Lf = L[:].rearrange("p a b -> p (a b)")
    S.mm(cs_ps[:, 0:128], ut[:], Lf, ["ut", "L"], [("qk", 0)])
    S.mm(tot_ps[:, 0:128], onesf[:], Lf, ["onesf", "L"], [("qk", 1)])
    tot = S.sb([128, NT, 2], F32, "tot")
    pa = S.sb([128, NT, 2], F32, "pa")
    pbuf = S.sb([128, NT, 2], F32, "pbuf")
    S.copy("dve", tot[:], tot_ps[:, 0:128].rearrange("p (a b) -> p a b", b=2), [("qk", 1)], ["tot"])
    S.copy("dve", pa[:], tot[:], ["tot"], ["pa"])
    cur, nxt, ck, nk = pa, pbuf, "pa", "pbuf"
    sh = 1
    while sh < NT:
        S.copy("dve", nxt[:, 0:sh, :], cur[:, 0:sh, :], [ck], [nk])
        S.tt("dve", nxt[:, sh:NT, :], cur[:, sh:NT, :], cur[:, 0:NT - sh, :], ALU.add, [ck], [nk])
        cur, nxt, ck, nk = nxt, cur, nk, ck
        sh *= 2
    incl, inclk = cur, ck
    c = S.sb([128, NT, 2], F32, "c")
    S.tt("dve", c[:], incl[:], tot[:], ALU.subtract, [inclk, "tot"], ["c0"])
    S.tt("dve", c[:], c[:], cs_ps[:, 0:128].rearrange("p (a b) -> p a b", b=2), ALU.add, ["c0", ("qk", 0)], ["c"])
    gam = S.sb([128, NT, 2], F32, "gam")
    i4 = incl[:].rearrange("p (a b) h -> p a b h", b=4)
    S.tt("dve", gam[:].rearrange("p (a b) h -> p a b h", b=4), i4, i4[:, :, 0:1, :].to_broadcast([128, NQB, 4, 2]),
         ALU.subtract, [inclk], ["gam0"])
    S.act(gam[:], gam[:], AF.Exp, ["gam0"], ["gam"])

    biasP = [S.sb([128, NT], F32, "biasP") for _ in range(2)]
    biasD = [S.sb([128, 4, 4], F32, "biasD") for _ in range(2)]
    NPT = 4
    PT = [S.sb([128, 512], BF16, "PT") for _ in range(NPT)]
    od = [S.sb([128, 129], F32, "od") for _ in range(4)]
    osb = [S.sb([128, 129], F32, "osb") for _ in range(4)]
    rcp = [S.sb([128, 1], F32, "rcp") for _ in range(4)]
    ysb = [S.sb([128, 128], BF16, "ysb") for _ in range(4)]
    ybuf = [S.sb([128, 512], BF16, "ybuf") for _ in range(2)]
    st_ps = [banks[0], banks[1]]
    accP = [banks[2][:, 0:129], banks[2][:, 256:385], banks[3][:, 0:129], banks[3][:, 256:385]]
    accD = [banks[4][:, 0:129], banks[4][:, 256:385], banks[5][:, 0:129], banks[5][:, 256:385]]
    groups = [(i, hh) for i in range(NQB) for hh in range(2)]
    tiles = []
    for gi, (i, hh) in enumerate(groups):
        seq = [("past", j) for j in range(4 * i)] + [("diag", jp) for jp in range(4)]
        for q, (kind, idx) in enumerate(seq):
            tiles.append((gi, kind, idx, q == 0, q == len(seq) - 1))

    def setup(gi):
        i, hh = groups[gi]
        bs = gi % 2
        npast = 4 * i
        if npast:
            S.ts("dve", biasP[bs][:, 0:npast], c[:, 0:npast, hh], -1.0, incl[:, 4 * i, hh:hh + 1], ALU.mult, ALU.add,
                 ["c", inclk], [("biasP", bs)])
        for u in range(4):
            S.ts("dve", biasD[bs][:, u, 0:u + 1], c[:, 4 * i:4 * i + u + 1, hh], -1.0,
                 incl[:, 4 * i + u, hh:hh + 1], ALU.mult, ALU.add, ["c", inclk], [("biasD", bs)])

    def emit_qk(k):
        gi, kind, idx, _, _ = tiles[k]
        i, hh = groups[gi]
        sb_ = k % 2
        qkeys = [("QKT", 4 * i + u) for u in range(4)]
        if kind == "past":
            S.mm(st_ps[sb_][:], QKT[:, 2 + hh, idx * 128:(idx + 1) * 128], QKT[:, hh, i * 512:(i + 1) * 512],
                 [("QKT", idx)] + qkeys, [("qk", sb_)])
        else:
            kt = 4 * i + idx
            w = (4 - idx) * 128
            S.mm(st_ps[sb_][:, 0:w], QKT[:, 2 + hh, kt * 128:(kt + 1) * 128],
                 QKT[:, hh, i * 512 + idx * 128:(i + 1) * 512], [("QKT", kt)] + qkeys, [("qk", sb_)])

    def emit_rest(k):
        gi, kind, idx, _, _ = tiles[k]
        i, hh = groups[gi]
        bs = gi % 2
        sb_ = k % 2
        pb_ = k % NPT
        npast = 4 * i
        if kind == "past":
            j = idx
            S.act(PT[pb_][:], st_ps[sb_][:], AF.Exp, [("qk", sb_), ("biasP", bs)],
                  [("PTd", pb_, u) for u in range(4)], bias=biasP[bs][:, j:j + 1], scale=scale)
            for u in range(4):
                S.mm(accP[u], PT[pb_][:, u * 128:(u + 1) * 128], V[:, j, hh, :], [("PTd", pb_, u), ("V", j), "Vones"],
                     [("accP", u)], start=(j == 0 and u % 2 == 0), stop=(j == npast - 1))
        else:
            jp = idx
            kt = 4 * i + jp
            for u in range(jp, 4):
                pk = ("PTd", pb_, u)
                S.act(PT[pb_][:, u * 128:(u + 1) * 128], st_ps[sb_][:, (u - jp) * 128:(u - jp + 1) * 128], AF.Exp,
                      [("qk", sb_), ("biasD", bs)], [pk], bias=biasD[bs][:, u, jp:jp + 1], scale=scale)
                if u == jp:
                    S.tt("pool", PT[pb_][:, u * 128:(u + 1) * 128], PT[pb_][:, u * 128:(u + 1) * 128], maskT[:],
                         ALU.mult, [pk, "maskT"], [pk])
                S.mm(accD[u], PT[pb_][:, u * 128:(u + 1) * 128], V[:, kt, hh, :], [pk, ("V", kt), "Vones"],
                     [("accD", u)], start=(jp == 0 and u % 2 == 0), stop=(jp == u))

    def emit_final(gi):
        i, hh = groups[gi]
        bs = gi % 2
        npast = 4 * i
        for u in range(4):
            if npast:
                S.copy("act", od[u][:], accD[u], [("accD", u)], [("od", u)])
            else:
                S.copy("act", osb[u][:], accD[u], [("accD", u)], [("osb", u)])
        if npast:
            for u in range(4):
                S.stt("dve", osb[u][:], accP[u], gam[:, 4 * i + u, hh:hh + 1], od[u][:], ALU.mult, ALU.add,
                      [("accP", u), "gam", ("od", u)], [("osb", u)])
        for u in range(4):
            S.recip(rcp[u][:], osb[u][:, 128:129], [("osb", u)], [("rcp", u)])
        for u in range(4):
            tq = 4 * i + u
            S.stt("dve", ysb[u][:], osb[u][:, 0:128], rcp[u][:, 0:1], G[:, tq, hh * 128:(hh + 1) * 128], ALU.mult, ALU.mult,
                  [("osb", u), ("rcp", u), ("G", tq)], [("ysb", u)])
        for u in range(4):
            ys = u % 2
            S.tr(tpbs[ys][:, (u // 2) * 128:(u // 2 + 1) * 128], ysb[u][:], identb[:], [("ysb", u), "identb"], [("ty", u)])
        for u in range(4):
            ys = u % 2
            S.copy("act", ybuf[bs][:, u * 128:(u + 1) * 128], tpbs[ys][:, (u // 2) * 128:(u // 2 + 1) * 128], [("ty", u)],
                   [("ybuf", bs, u)])
        S.store(y1T.tile(i)[0][hh * 128:(hh + 1) * 128, :], ybuf[bs][:],
                [("ybuf", bs, u) for u in range(4)], "st_y%d" % bs)

    def issue_ag(kc):
        S.P.op("pool", lambda e: e.collective_compute("AllGather", ALU.bypass, replica_groups=RG,
                                                      ins=[y1T.t[kc].opt()], outs=[ag_dst.t[kc].opt()]),
               [("out", "st_y0"), ("out", "st_y1")], [("agout", kc)], dma="cc%d" % kc, inc=1)

    NTL = len(tiles)
    setup(0)
    emit_qk(0)
    ag_wait = []
    for k in range(NTL):
        if k + 1 < NTL:
            if tiles[k + 1][3]:
                setup(tiles[k + 1][0])
            emit_qk(k + 1)
        emit_rest(k)
        if tiles[k][4]:
            gi = tiles[k][0]
            emit_final(gi)
            i_, hh_ = groups[gi]
            if ag_dst is not None and hh_ == 1 and i_ % 4 == 3 and i_ // 4 < NCHK - 1:
                ag_wait.append([i_ // 4, 6])
        for aw in list(ag_wait):
            aw[1] -= 1
            if aw[1] <= 0 or k == NTL - 1:
                issue_ag(aw[0])
                ag_wait.remove(aw)
    if ag_dst is not None:
        S.P.op("pool", None, [("agout", k) for k in range(NCHK - 1)], ())
    return S.finish()


_CACHE = {}


def _get(name, fn):
    if name not in _CACHE:
        _CACHE[name] = fn()
    return _CACHE[name]


def _consts():
    ident = np.eye(128, dtype=np.float32)
    k = np.arange(128)
    ut = (k[:, None] <= k[None, :]).astype(np.float32)
    return ident, ut


def _run(nc, in_maps):
    res = run_bass_kernel_spmd(nc, in_maps, core_ids=list(range(NCORE)))
    return res.results


def fox_inputs(xnT_b, b_w_in, b_f_bias, b_q_norm_w, b_k_norm_w, g):
    h0, h1 = 2 * g, 2 * g + 1
    cols = []
    for base in (0, 1024, 2048, 3072):
        for h in (h0, h1):
            cols.extend(range(base + h * 128, base + (h + 1) * 128))
    cols.extend([4096 + h0, 4096 + h1])
    ident, ut = _consts()
    return {
        "win": np.ascontiguousarray(b_w_in[0][:, cols]),
        "fbias": np.ascontiguousarray(b_f_bias[0][[h0, h1]].reshape(1, 2)),
        "qkw": np.ascontiguousarray(np.concatenate([b_q_norm_w[0], b_q_norm_w[0], b_k_norm_w[0], b_k_norm_w[0]]).reshape(1, 512)),
        "ident": ident, "ut": ut,
    }


BIG = 30000.0


class _Stop(Exception):
    pass


def build_gdn_stage(S, xT, win, normw, convw, hconst, onw_d, cm, sw_d, y0T, nsup=T // 512, dbg=99, ag_dst=None):

    def ck(level):
        if dbg < level:
            raise _Stop()

    nw = S.sb([128, 8], F32, "nw")
    S.load(nw[:], normw, ["normw"], "ld_nw")
    wb, wkeys = S.load_weight_bf16(win, 1028, "wi", scale_col=nw, scale_mul=1.0)
    cmat = S.sb([128, 7, 128], F32, "cmat")
    S.load(cmat[:], cm.rearrange("a p f -> p a f"), ["cmat"], "ld_cm")
    identf, maskS, maskTn, LTbd, ONESbd, SEL0, SEL1 = [cmat[:, a, :] for a in range(7)]
    swm = S.sb([128, 128], F32, "swm")
    S.load(swm[:], sw_d, ["swm"], "ld_sw")
    cw = S.sb([128, 24], F32, "cw")
    S.load(cw[:], convw, ["cw"], "ld_cw")
    hc = S.sb([128, 2], F32, "hc")
    S.load(hc[:], hconst, ["hc"], "ld_hc")
    onwb = S.sb([128, 128], F32, "onwb")
    S.load(onwb[:], onw_d.partition_broadcast(128), ["onwb"], "ld_onw")
    identb = S.sb([128, 128], BF16, "identb")
    S.copy("dve", identb[:], identf, ["cmat"], ["identb"])
    onesb = S.sb([128, 128], BF16, "onesb")
    S.memset("pool", onesb[:], 1.0, ["onesb"])
    onesf = S.sb([128, 128], F32, "onesf")
    S.memset("pool", onesf[:], 1.0, ["onesf"])
    DW = S.sb([128, 24, 128], BF16, "DW")
    for ct in range(24):
        S.ts("dve" if ct % 2 else "pool", DW[:, ct, :], identf, cw[:, ct:ct + 1], None, ALU.mult, None, ["cmat", "cw"], ["DW"])
    nexpA = S.sb([128, 1], F32, "nexpA")
    S.act(nexpA[:], hc[:, 0:1], AF.Exp, ["hc"], ["nexpA0"])
    S.ts("dve", nexpA[:], nexpA[:], -1.0, None, ALU.mult, None, ["nexpA0"], ["nexpA"])

    xv = xT.rearrange("(fc p) t -> p fc t", p=128)
    xs = [S.sb([128, 8, 512], F32, "xs") for _ in range(2)]
    xb = [S.sb([128, 8, 512], BF16, "xb") for _ in range(2)]
    xsq = [S.sb([128, 8, 512], BF16, "xsq") for _ in range(2)]
    lt = S.sb([128, 512], F32, "lt")
    rstd = [S.sb([128, 512], F32, "rstd") for _ in range(2)]
    prebf = [S.sb([128, 515], BF16, "prebf") for _ in range(6)]
    for cc in range(6):
        S.memset("pool", prebf[cc][:], 0.0, [("prebf", cc)])
    qks = [S.sb([128, 512], F32, "qks") for _ in range(4)]
    sq2 = [S.sb([128, 512], BF16, "sq2") for _ in range(4)]
    lt2 = S.sb([128, 512], F32, "lt2")
    r2 = [S.sb([128, 512], F32, "r2") for _ in range(2)]
    zp = [S.sb([128, 512], F32, "zp") for _ in range(2)]
    fm = {nm: [S.sb([128, 8, 2, 64], BF16, nm) for _ in range(2)] for nm in ("qT2", "kT2", "vT2", "zT2")}
    ybuf = [S.sb([128, 2, 8, 64], BF16, "ybuf") for _ in range(2)]
    BG = S.sb([128, 8, 2], F32, "BG")
    tmsb = S.sb([128, 4], F32, "tmsb")
    rcol = S.sb([128, 1], F32, "rcol")
    lcol = S.sb([128, 1], F32, "lcol")
    sc = {nm: S.sb([128, 8], F32, nm) for nm in ("beta", "nbeta", "xa", "ab", "sp", "glog", "eg", "beg", "dd", "ek")}
    gs = S.sb([128, 4, 8], F32, "gs")
    ge = S.sb([128, 2, 8], F32, "ge")

    FA = [S.bank(), S.bank()]
    FB = [S.bank(), S.bank()]
    H = [S.bank(BF16), S.bank(BF16)]
    FS = S.bank()
    HS = S.bank(BF16)
    pre_ps = [FA[0], FA[1]]
    cv_ps = FB[0]
    ss_ps = FB[1]
    sm_ps = FS[:, 0:128]

    NPS = 4

    def mk(nm, dt=BF16, shape=(128, 128)):
        return S.sb(list(shape), dt, nm)
    W = []
    for q in range(2):
        W.append(dict(kbeg=mk("kbeg"), dg=mk("dg", F32), a1=mk("a1", F32), a2=mk("a2", F32), Ds=mk("Ds", F32),
                      DTi=mk("DTi", F32), Erow=mk("Erow", F32), X=mk("X"), XT=mk("XT"),
                      Pb=[mk("Pb"), mk("Pb")], Yb=[mk("Yb"), mk("Yb")], TTb=[mk("TTb"), mk("TTb")]))
    R = []
    for p_ in range(NPS):
        d_ = dict(TT=mk("TT"), vb=mk("vb"), nw0=mk("nw0"), nw1=mk("nw1"), qd0=mk("qd0"), qd1=mk("qd1"),
                  attnT=mk("attnT"), kend0=mk("kend0"), kend1=mk("kend1"), gw=mk("gw", F32))
        for nm in ("nw0", "nw1", "qd0", "qd1", "kend0", "kend1"):
            S.memset("pool", d_[nm][:], 0.0, [(nm, p_)])
        R.append(d_)
    vnb = mk("vnb")
    Sf = mk("Sf", F32, (128, 256))
    Sbf = mk("Sbf", BF16, (128, 256))
    S.memset("pool", Sf[:], 0.0, ["Sf"])
    S.memset("pool", Sbf[:], 0.0, ["Sbf"])
    osq = mk("osq", F32)
    oss = S.sb([128, 1], F32, "oss")
    ol = S.sb([128, 1], F32, "ol")
    orstd = S.sb([128, 1], F32, "orstd")
    ysb = mk("ysb")

    def ldx(s):
        S.load(xs[s % 2][:], xv[:, :, s * 512:(s + 1) * 512], [("xs", s % 2)], "ld_x%d" % (s % 2))

    def front(s):
        sl = s % 2
        xsk = ("xs", sl)
        S.copy("pool", xb[sl][:], xs[sl][:], [xsk], [("xb", sl)])
        S.act(xsq[sl][:], xs[sl][:], AF.Square, [xsk], [("xsq", sl)])
        for fc in range(8):
            S.mm(ss_ps[:], onesb[:], xsq[sl][:, fc, :], ["onesb", ("xsq", sl)], ["ss"], start=(fc == 0), stop=(fc == 7))
        S.rsqrt(rstd[sl][:], ss_ps[:], ["ss"], [("rstd", sl)], 1.0 / D, EPS, lt[:])

    def st0(s, cc):
        sl = s % 2
        p = pre_ps[cc % 2]
        pk = ("pre", cc % 2)
        for fc in range(8):
            S.mm(p[:], wb[:, fc, cc * 128:(cc + 1) * 128], xb[sl][:, fc, :], [wkeys[fc], ("xb", sl)], [pk],
                 start=(fc == 0), stop=(fc == 7))
        if cc < 6:
            S.tt("dve", prebf[cc][:, 3:515], p[:], rstd[sl][:], ALU.mult, [pk, ("rstd", sl)], [("prebf", cc)])
        else:
            S.tt("dve", zp[cc % 2][:], p[:], rstd[sl][:], ALU.mult, [pk, ("rstd", sl)], [("zp", cc % 2)])

    def st1(s, cc):
        sl = s % 2
        h = cc % 2
        typ = cc // 2
        b2 = cc % 2
        if cc < 6:
            bk = ("prebf", cc)
            for tap in range(4):
                S.mm(cv_ps[:], DW[:, cc * 4 + tap, :], prebf[cc][:, tap:tap + 512], ["DW", bk], ["cv"],
                     start=(tap == 0), stop=(tap == 3))
            S.copy("pool", prebf[cc][:, 0:3], prebf[cc][:, 512:515], [bk], [bk])
            if typ == 2:
                S.act(fm["vT2"][sl][:, :, h, :], cv_ps[:].rearrange("p (n i) -> p n i", n=8), AF.Silu,
                      ["cv"], [("vT2", sl, h)])
            else:
                S.act(qks[cc][:], cv_ps[:], AF.Silu, ["cv"], [("qks", cc)])
                S.act(sq2[cc][:], qks[cc][:], AF.Square, [("qks", cc)], [("sq2", cc)])
        else:
            S.act(fm["zT2"][sl][:, :, h, :], zp[b2][:].rearrange("p (n i) -> p n i", n=8), AF.Silu,
                  [("zp", b2)], [("zT2", sl, h)])

    def st2(s, cc):
        sl = s % 2
        h = cc % 2
        typ = cc // 2
        b2 = cc % 2
        if typ > 1:
            return
        S.mm(ss_ps[:], onesb[:], sq2[cc][:], ["onesb", ("sq2", cc)], ["ss"])
        S.rsqrt(r2[b2][:], ss_ps[:], ["ss"], [("r2", b2)], 1.0, EPS, lt2[:])
        nm = "qT2" if typ == 0 else "kT2"
        q3 = qks[cc][:].rearrange("p (n i) -> p n i", n=8)
        r3 = r2[b2][:].rearrange("p (n i) -> p n i", n=8)
        if typ == 0:
            S.stt("dve", fm[nm][sl][:, :, h, :], q3, 128.0 ** -0.5, r3, ALU.mult, ALU.mult,
                  [("qks", cc), ("r2", b2)], [(nm, sl, h)])
        else:
            S.tt("dve", fm[nm][sl][:, :, h, :], q3, r3, ALU.mult, [("qks", cc), ("r2", b2)], [(nm, sl, h)])

    def tmstep(s, jj):
        sl = s % 2
        tm_ps = sm_ps[:, 0:5]
        sw_ps = sm_ps[:, 8:12]
        for fc in range(8):
            S.mm(tm_ps[:, 0:4], xb[sl][:, fc, jj * 128:(jj + 1) * 128], wb[:, fc, 1024:1028], [("xb", sl), wkeys[fc]], ["tm"],
                 start=(fc == 0), stop=(fc == 7))
        for fc in range(8):
            S.mm(tm_ps[:, 4:5], xsq[sl][:, fc, jj * 128:(jj + 1) * 128], onesb[:, 0:1], [("xsq", sl), "onesb"], ["tmss"],
                 start=(fc == 0), stop=(fc == 7))
        S.rsqrt(rcol[:], tm_ps[:, 4:5], ["tmss"], ["rcol"], 1.0 / D, EPS, lcol[:])
        S.ts("dve", tmsb[:], tm_ps[:, 0:4], rcol[:, 0:1], None, ALU.mult, None, ["tm", "rcol"], ["tmsb"])
        S.mm(sw_ps, swm[:], tmsb[:], ["swm", "tmsb"], ["sw"])
        n0, n1 = 2 * jj, 2 * jj + 1
        S.copy("pool", BG[0:64, n0, :], tmsb[0:64, 0:2], ["tmsb"], [("BG", jj, 0)])
        S.copy("dve", BG[64:128, n0, :], sw_ps[64:128, 2:4], ["sw"], [("BG", jj, 1)])
        S.copy("dve", BG[0:64, n1, :], sw_ps[0:64, 0:2], ["sw"], [("BG", jj, 2)])
        S.copy("pool", BG[64:128, n1, :], tmsb[64:128, 2:4], ["tmsb"], [("BG", jj, 3)])

    def scalars(s):
        sl = s % 2
        bgk = [("BG", jj, q) for jj in range(4) for q in range(4)]
        S.act(sc["beta"][:], BG[:, :, 0], AF.Exp, bgk, ["beta0"], scale=-1.0)
        S.ts("dve", sc["beta"][:], sc["beta"][:], 1.0, None, ALU.add, None, ["beta0"], ["beta1"])
        S.recip(sc["beta"][:], sc["beta"][:], ["beta1"], ["beta"])
        S.ts("dve", sc["nbeta"][:], sc["beta"][:], -1.0, None, ALU.mult, None, ["beta"], ["nbeta"])
        S.ts("dve", sc["xa"][:], BG[:, :, 1], hc[:, 1:2], None, ALU.add, None, bgk + ["hc"], ["xa"])
        S.act(sc["ab"][:], sc["xa"][:], AF.Abs, ["xa"], ["ab0"])
        S.act(sc["ab"][:], sc["ab"][:], AF.Exp, ["ab0"], ["ab1"], scale=-1.0)
        S.act(sc["ab"][:], sc["ab"][:], AF.Ln, ["ab1"], ["ab"], bias=1.0)
        S.ts("dve", sc["sp"][:], sc["xa"][:], 0.0, None, ALU.max, None, ["xa"], ["sp0"])
        S.tt("dve", sc["sp"][:], sc["sp"][:], sc["ab"][:], ALU.add, ["sp0", "ab"], ["sp"])
        S.ts("dve", sc["glog"][:], sc["sp"][:], nexpA[:, 0:1], None, ALU.mult, None, ["sp", "nexpA"], ["glog"])
        gps = sm_ps[:, 16:48].rearrange("p (a b) -> p a b", a=4)
        for a, m in enumerate((LTbd, ONESbd, SEL0, SEL1)):
            S.mm(gps[:, a, :], m, sc["glog"][:], ["cmat", "glog"], ["gps"])
        S.copy("dve", gs[:], gps, ["gps"], ["gs"])
        S.act(sc["eg"][:], gs[:, 0, :], AF.Exp, ["gs"], ["eg"])
        S.tt("dve", sc["beg"][:], sc["beta"][:], sc["eg"][:], ALU.mult, ["beta", "eg"], ["beg"])
        S.tt("dve", sc["dd"][:], gs[:, 1, :], gs[:, 0, :], ALU.subtract, ["gs"], ["dd"])
        S.act(sc["ek"][:], sc["dd"][:], AF.Exp, ["dd"], ["ek"])
        S.act(ge[:], gs[:, 2:4, :], AF.Exp, ["gs"], ["ge"])


    def phase_a(s):
        for k in range(9):
            if k < 8:
                st0(s, k)
            if 1 <= k <= 8:
                st1(s, k - 1)
        for k in range(4):
            st2(s, k)
            tmstep(s, k)
        scalars(s)

    def par(s, n):
        sl = s % 2
        q = n % 2
        p_ = n % NPS
        w, r = W[q], R[p_]
        G_ps, KK_ps, KQ_ps, wT_ps = FA[q][:, 0:128], FA[q][:, 128:256], FA[q][:, 256:384], FA[q][:, 384:512]
        P_ps, Y_ps, Tn_ps = FB[q][:, 0:128], FB[q][:, 128:256], FB[q][:, 256:384]
        trb = H[q]
        kTn = fm["kT2"][sl][:, n, :, :].rearrange("p h i -> p (h i)")
        qTn = fm["qT2"][sl][:, n, :, :].rearrange("p h i -> p (h i)")
        vTn = fm["vT2"][sl][:, n, :, :].rearrange("p h i -> p (h i)")
        zTn = fm["zT2"][sl][:, n, :, :].rearrange("p h i -> p (h i)")
        kk = [("kT2", sl, 0), ("kT2", sl, 1)]
        qk_ = [("qT2", sl, 0), ("qT2", sl, 1)]
        vk = [("vT2", sl, 0), ("vT2", sl, 1)]
        zk = [("zT2", sl, 0), ("zT2", sl, 1)]
        K_ = lambda nm: (nm, q)
        Rk = lambda nm: (nm, p_)
        col = lambda t_: t_[:, n:n + 1]
        S.tr(trb[:, 0:128], kTn, identb[:], kk + ["identb"], [K_("trk")])
        S.tr(trb[:, 128:256], vTn, identb[:], vk + ["identb"], [K_("trv")])
        S.tr(trb[:, 256:384], zTn, identb[:], zk + ["identb"], [K_("trz")])
        S.ts("pool", w["dg"][:], identf, gs[:, 0, n:n + 1], None, ALU.mult, None, ["cmat", "gs"], [K_("dg")])
        S.mm(G_ps, onesf[:], w["dg"][:], ["onesf", K_("dg")], [K_("G")])
        S.mm(KK_ps, kTn, kTn, kk, [K_("KK")])
        S.mm(KQ_ps, kTn, qTn, kk + qk_, [K_("KQ")])
        yield
        S.ts("dve", w["kbeg"][:], trb[:, 0:128], col(sc["beg"]), None, ALU.mult, None, [K_("trk"), "beg"], [K_("kbeg")])
        S.act(r["kend0"][0:64, :], trb[0:64, 0:128], AF.Copy, [K_("trk"), "ek"], [Rk("kend0")], scale=sc["ek"][0:64, n:n + 1])
        S.act(r["kend1"][64:128, :], trb[64:128, 0:128], AF.Copy, [K_("trk"), "ek"], [Rk("kend1")], scale=sc["ek"][64:128, n:n + 1])
        yield
        S.act(r["vb"][:], trb[:, 128:256], AF.Copy, [K_("trv"), "beta"], [Rk("vb")], scale=col(sc["beta"]))
        S.tt("dve", r["gw"][:], trb[:, 256:384], onwb[:], ALU.mult, [K_("trz"), "onwb"], [Rk("gw")])
        yield
        S.stt("dve", w["a1"][:], G_ps, gs[:, 0, n:n + 1], maskS, ALU.subtract, ALU.add, [K_("G"), "gs", "cmat"], [K_("a1")])
        S.act(w["Erow"][:], G_ps, AF.Exp, [K_("G")], [K_("Erow")])
        yield
        S.stt("dve", w["a2"][:], G_ps, gs[:, 0, n:n + 1], maskTn, ALU.subtract, ALU.add, [K_("G"), "gs", "cmat"], [K_("a2")])
        S.act(w["Ds"][:], w["a1"][:], AF.Exp, [K_("a1")], [K_("Ds")], scale=-1.0)
        yield
        S.act(w["DTi"][:], w["a2"][:], AF.Exp, [K_("a2")], [K_("DTi")])
        S.stt("dve", w["X"][:], KK_ps, col(sc["nbeta"]), w["Ds"][:], ALU.mult, ALU.mult, [K_("KK"), "nbeta", K_("Ds")], [K_("X")])
        yield
        S.tr(trb[:, 384:512], w["X"][:], identb[:], [K_("X"), "identb"], [K_("trx")])
        S.tt("dve", r["attnT"][:], KQ_ps, w["DTi"][:], ALU.mult, [K_("KQ"), K_("DTi")], [Rk("attnT")])
        yield
        S.copy("act", w["XT"][:], trb[:, 384:512], [K_("trx")], [K_("XT")])
        S.tt("pool", r["qd0"][:, 0:64], qTn[:, 0:64], w["Erow"][:, 0:64], ALU.mult, qk_ + [K_("Erow")], [Rk("qd0")])
        S.tt("pool", r["qd1"][:, 64:128], qTn[:, 64:128], w["Erow"][:, 64:128], ALU.mult, qk_ + [K_("Erow")], [Rk("qd1")])
        yield
        S.tt("pool", w["TTb"][0][:], identb[:], w["XT"][:], ALU.add, ["identb", K_("XT")], [K_("TT0")])
        Pc, Pk, Yc, Yk = w["X"], K_("X"), w["XT"], K_("XT")
        ti = 0
        pend = None
        for k in range(1, 6):
            pi = k % 2
            S.mm(P_ps, Yc[:], Pc[:], [Yk, Pk], [K_("Pps")])
            if k < 5:
                S.mm(Y_ps, Pc[:], Yc[:], [Yk, Pk], [K_("Yps")])
            if pend is not None:
                S.mm(Tn_ps, pend[0][:], w["TTb"][ti][:], [pend[1], K_("TT%d" % ti)], [K_("Tps")])
            yield
            S.copy("act", w["Pb"][pi][:], P_ps, [K_("Pps")], [K_("Pb%d" % pi)])
            if k < 5:
                S.copy("dve", w["Yb"][pi][:], Y_ps, [K_("Yps")], [K_("Yb%d" % pi)])
            if pend is not None:
                S.tt("dve", w["TTb"][1 - ti][:], w["TTb"][ti][:], Tn_ps, ALU.add, [K_("TT%d" % ti), K_("Tps")], [K_("TT%d" % (1 - ti))])
                ti = 1 - ti
            pend = (w["Pb"][pi], K_("Pb%d" % pi))
            Pc, Pk, Yc, Yk = w["Pb"][pi], K_("Pb%d" % pi), w["Yb"][pi], K_("Yb%d" % pi)
            yield
        S.mm(Tn_ps, pend[0][:], w["TTb"][ti][:], [pend[1], K_("TT%d" % ti)], [K_("Tps")])
        yield
        S.tt("dve", r["TT"][:], w["TTb"][ti][:], Tn_ps, ALU.add, [K_("TT%d" % ti), K_("Tps")], [Rk("TT")])
        yield
        S.mm(wT_ps, w["kbeg"][:], r["TT"][:], [K_("kbeg"), Rk("TT")], [K_("wT")])
        yield
        S.ts("dve", r["nw0"][:, 0:64], wT_ps[:, 0:64], -1.0, None, ALU.mult, None, [K_("wT")], [Rk("nw0")])
        S.act(r["nw1"][:, 64:128], wT_ps[:, 64:128], AF.Copy, [K_("wT")], [Rk("nw1")], scale=-1.0)
        yield

    def scan(s, n):
        sl = s % 2
        p_ = n % NPS
        r = R[p_]
        Rk = lambda nm: (nm, p_)
        vn_ps, o_ps, kv_ps = FS[:, 0:128], FS[:, 128:256], FS[:, 256:512]
        S.mm(vn_ps, r["TT"][:], r["vb"][:], [Rk("TT"), Rk("vb")], ["vn"], start=True, stop=False)
        S.mm(vn_ps, r["nw0"][:], Sbf[:, 0:128], [Rk("nw0"), "Sbf"], ["vn"], start=False, stop=False)
        S.mm(vn_ps, r["nw1"][:], Sbf[:, 128:256], [Rk("nw1"), "Sbf"], ["vn"], start=False, stop=True)
        yield
        S.copy("act", vnb[:], vn_ps, ["vn"], ["vnb"])
        yield
        S.mm(o_ps, r["qd0"][:], Sbf[:, 0:128], [Rk("qd0"), "Sbf"], ["o"], start=True, stop=False)
        S.mm(o_ps, r["qd1"][:], Sbf[:, 128:256], [Rk("qd1"), "Sbf"], ["o"], start=False, stop=False)
        S.mm(o_ps, r["attnT"][:], vnb[:], [Rk("attnT"), "vnb"], ["o"], start=False, stop=True)
        S.mm(kv_ps[:, 0:128], r["kend0"][:], vnb[:], [Rk("kend0"), "vnb"], ["kv"])
        S.mm(kv_ps[:, 128:256], r["kend1"][:], vnb[:], [Rk("kend1"), "vnb"], ["kv"])
        yield
        for hh in range(2):
            S.stt("dve", Sf[:, hh * 128:(hh + 1) * 128], Sf[:, hh * 128:(hh + 1) * 128], ge[:, hh, n:n + 1],
                  kv_ps[:, hh * 128:(hh + 1) * 128], ALU.mult, ALU.add, ["Sf", "ge", "kv"], ["Sf"])
        S.act(osq[:], o_ps, AF.Square, ["o"], ["osq"])
        yield
        S.copy("pool", Sbf[:], Sf[:], ["Sf"], ["Sbf"])
        S.P.op("dve", lambda e: e.reduce_sum(out=oss[:], in_=osq[:], axis=AX.X), ["osq"], ["oss"])
        yield
        S.rsqrt(orstd[:], oss[:], ["oss"], ["orstd"], 1.0 / 128, EPS, ol[:])
        yield
        S.stt("dve", ysb[:], o_ps, orstd[:, 0:1], r["gw"][:], ALU.mult, ALU.mult, ["o", "orstd", Rk("gw")], ["ysb"])
        yield
        S.tr(HS[:, 0:128], ysb[:], identb[:], ["ysb", "identb"], ["try"])
        yield
        S.copy("act", ybuf[sl][:, :, n, :], HS[:, 0:128].rearrange("p (h i) -> p h i", h=2), ["try"], [("ybuf", sl, n)])
        yield

    def drive(gens):
        gens = list(gens)
        while gens:
            for g_ in list(gens):
                try:
                    next(g_)
                except StopIteration:
                    gens.remove(g_)

    def chain(*gs_):
        for g_ in gs_:
            yield from g_

    def phase_b_all(s):
        for kp in range(5):
            gens = []
            if kp < 4:
                gens += [par(s, 2 * kp), par(s, 2 * kp + 1)]
            if kp >= 1:
                gens.append(chain(scan(s, 2 * kp - 2), scan(s, 2 * kp - 1)))
            drive(gens)

    try:
        ck(0)
        ldx(0)
        ck(1)
        front(0)

        def issue_ag(k):
            keys = [("out", "st_y%d%d" % (a_, b_)) for a_ in range(2) for b_ in range(2)]
            S.P.op("pool", lambda e: e.collective_compute("AllGather", ALU.bypass, replica_groups=RG,
                                                          ins=[y0T.t[k].opt()], outs=[ag_dst.t[k].opt()]),
                   keys, [("agout", k)], dma="cc%d" % k, inc=1)

        for s in range(nsup):
            if s + 1 < nsup:
                ldx(s + 1)
            phase_a(s)
            if ag_dst is not None and s >= 4 and s % 4 == 0:
                issue_ag(s // 4 - 1)
            if s + 1 < nsup:
                front(s + 1)
            phase_b_all(s)
            for hh in range(2):
                S.store(y0T.tile(s)[0][hh * 128:(hh + 1) * 128, :],
                        ybuf[s % 2][:, hh, :, :].rearrange("p n i -> p (n i)"),
                        [("ybuf", s % 2, n) for n in range(8)], "st_y%d%d" % (s % 2, hh))
        if ag_dst is not None:
            S.P.op("pool", None, [("agout", k) for k in range(NCHK - 1)], ())
    except _Stop:
        pass
    return S.finish()


def gdn_inputs(xT_b, a_norm_w, a_w_in, a_conv_w, a_A_log, a_dt_bias, a_o_norm_w, g):
    h0, h1 = 2 * g, 2 * g + 1
    cols, ccols = [], []
    for base in (0, 1024, 2048, 3072):
        for h in (h0, h1):
            cols.extend(range(base + h * 128, base + (h + 1) * 128))
            if base < 3072:
                ccols.append((base + h * 128, base + (h + 1) * 128))
    cols.extend([4096 + h0, 4104 + h0, 4096 + h1, 4104 + h1])
    cw = np.stack([a_conv_w[0][:, a:b].T for (a, b) in ccols], axis=1)
    hconst = np.zeros((128, 2), np.float32)
    hconst[0:64, 0], hconst[64:128, 0] = a_A_log[0][h0], a_A_log[0][h1]
    hconst[0:64, 1], hconst[64:128, 1] = a_dt_bias[0][h0], a_dt_bias[0][h1]
    p = np.arange(128)
    hh, ii = p // 64, p % 64
    same = hh[:, None] == hh[None, :]
    ident = np.eye(128, dtype=np.float32)
    maskS = np.where(same & (ii[:, None] > ii[None, :]), 0.0, BIG).astype(np.float32)
    maskTn = np.where(same & (ii[None, :] >= ii[:, None]), 0.0, -BIG).astype(np.float32)
    LTbd = (same & (ii[:, None] <= ii[None, :])).astype(np.float32)
    ONESbd = same.astype(np.float32)
    SEL0 = np.repeat((hh == 0)[:, None], 128, 1).astype(np.float32)
    SEL1 = np.repeat((hh == 1)[:, None], 128, 1).astype(np.float32)
    swm = (p[:, None] == ((p[None, :] + 64) % 128)).astype(np.float32)
    return {
        "xT": xT_b,
        "win": np.ascontiguousarray(a_w_in[0][:, cols]),
        "normw": np.ascontiguousarray(a_norm_w[0].reshape(8, 128).T),
        "convw": np.ascontiguousarray(cw.reshape(128, 24)),
        "hconst": hconst,
        "onw": np.ascontiguousarray(a_o_norm_w[0].reshape(1, 128)),
        "cmats": np.stack([ident, maskS, maskTn, LTbd, ONESbd, SEL0, SEL1]),
        "swm": swm,
    }


def build_fused(upto="f", gdn_nsup=T // 512):
    nc = bass.Bass("TRN2", target_bir_lowering=False)

    def din(name, shape, dt=F32):
        return nc.dram_tensor(name, list(shape), dt, kind="ExternalInput").ap()

    def dint(name, shape, dt):
        return nc.dram_tensor(name, list(shape), dt).ap()

    xT = din("xT", [D, T]); xTq = din("xTq", [256, T])
    a_win = din("a_win", [D, 1028]); a_normw = din("a_normw", [128, 8]); convw = din("convw", [128, 24])
    hconst = din("hconst", [128, 2]); onw = din("onw", [1, 128]); cmats = din("cmats", [7, 128, 128])
    swm = din("swm", [128, 128])
    a_wo = din("a_wo", [D, 256]); b_normw2 = din("b_normw2", [128, 2])
    b_win = din("b_win", [D, 1026]); fbias = din("fbias", [1, 2]); qkw = din("qkw", [1, 512]); ut = din("ut", [128, 128])
    b_wo = din("b_wo", [D, 256]); f_normw2 = din("f_normw2", [128, 2])
    outT = nc.dram_tensor("outT", [256, T], F32, kind="ExternalOutput").ap()

    y0_src = Chunked(nc, "y0_src", 256, BF16); y0_all = Chunked(nc, "y0_all", D, BF16)
    h1 = dint("h1", [256, T], F32); ss1_src = dint("ss1_src", [1, T], F32); ss1_all = dint("ss1_all", [4, T], F32)
    xn_src = Chunked(nc, "xn_src", 256, BF16); xn_all = Chunked(nc, "xn_all", D, BF16)
    y1_src = Chunked(nc, "y1_src", 256, BF16); y1_all = Chunked(nc, "y1_all", D, BF16)
    h2 = dint("h2", [256, T], F32); ss2_src = dint("ss2_src", [1, T], F32); ss2_all = dint("ss2_all", [4, T], F32)

    S = Stage(nc, "a_")
    build_gdn_stage(S, xT, a_win, a_normw, convw, hconst, onw, cmats, swm, y0_src, nsup=gdn_nsup, ag_dst=y0_all)
    cc_a = [S.P.dsem["cc%d" % k] for k in range(NCHK - 1)] if gdn_nsup == T // 512 else None
    if upto == "a":
        return nc

    S = Stage(nc, "b_")
    for k in range(NCHK - 1):
        S.P.ext(("ag", k), cc_a[k])
    allgather(S, y0_src.t[NCHK - 1], y0_all.t[NCHK - 1], ("ag", NCHK - 1), "cc%d" % (NCHK - 1))
    build_rows_proj(S, y0_all, xTq, a_wo, h1, ss1_src)
    S.finish()
    if upto == "b":
        return nc

    S = Stage(nc, "c_")
    allgather(S, ss1_src, ss1_all, "ag", "cc")
    build_rows_norm(S, h1, ss1_all, b_normw2, xn_src, BF16, in_key=["ag"])
    S.finish()
    if upto == "c":
        return nc

    S = Stage(nc, "d_")
    allgather_chunked(S, xn_src, xn_all)
    build_fox_stage(S, xn_all, b_win, fbias, qkw, cmats[0], ut, y1_src, ag_dst=y1_all)
    cc_d = [S.P.dsem["cc%d" % k] for k in range(NCHK - 1)]
    if upto == "d":
        return nc

    S = Stage(nc, "e_")
    for k in range(NCHK - 1):
        S.P.ext(("ag", k), cc_d[k])
    allgather(S, y1_src.t[NCHK - 1], y1_all.t[NCHK - 1], ("ag", NCHK - 1), "cc%d" % (NCHK - 1))
    build_rows_proj(S, y1_all, h1, b_wo, h2, ss2_src)
    S.finish()

    S = Stage(nc, "f_")
    allgather(S, ss2_src, ss2_all, "ag", "cc")
    build_rows_norm(S, h2, ss2_all, f_normw2, outT, F32, in_key=["ag"])
    S.finish()
    return nc


def kernel(x, a_norm_w, a_w_in, a_conv_w, a_A_log, a_dt_bias, a_o_norm_w, a_w_out,
           b_norm_w, b_w_in, b_f_bias, b_q_norm_w, b_k_norm_w, b_w_out, final_norm_w):
    f32 = lambda a: np.ascontiguousarray(np.asarray(a, dtype=np.float32))
    x = f32(x)
    (a_norm_w, a_w_in, a_conv_w, a_A_log, a_dt_bias, a_o_norm_w, a_w_out, b_norm_w, b_w_in, b_f_bias,
     b_q_norm_w, b_k_norm_w, b_w_out, final_norm_w) = [f32(a) for a in (
        a_norm_w, a_w_in, a_conv_w, a_A_log, a_dt_bias, a_o_norm_w, a_w_out, b_norm_w, b_w_in, b_f_bias,
        b_q_norm_w, b_k_norm_w, b_w_out, final_norm_w)]
    B = x.shape[0]
    xT = [np.ascontiguousarray(x[b].T) for b in range(B)]
    nc = _get("fused", build_fused)
    maps = []
    for c in range(NCORE):
        b, g = c // 4, c % 4
        fs = slice(g * 256, (g + 1) * 256)
        m = gdn_inputs(xT[b], a_norm_w, a_w_in, a_conv_w, a_A_log, a_dt_bias, a_o_norm_w, g)
        m["a_win"] = m.pop("win")
        m["a_normw"] = m.pop("normw")
        fi = fox_inputs(None, b_w_in, b_f_bias, b_q_norm_w, b_k_norm_w, g)
        m.update({
            "xTq": np.ascontiguousarray(xT[b][fs]),
            "a_wo": np.ascontiguousarray(a_w_out[0][:, fs]),
            "b_normw2": np.ascontiguousarray(b_norm_w[0][fs].reshape(2, 128).T),
            "b_win": fi["win"], "fbias": fi["fbias"], "qkw": fi["qkw"], "ut": fi["ut"],
            "b_wo": np.ascontiguousarray(b_w_out[0][:, fs]),
            "f_normw2": np.ascontiguousarray(final_norm_w[fs].reshape(2, 128).T),
        })
        maps.append(m)
    res = _run(nc, maps)
    out = np.empty((B, T, D), np.float32)
    for c in range(NCORE):
        b, g = c // 4, c % 4
        out[b, :, g * 256:(g + 1) * 256] = res[c]["outT"].T
    return out
```
